# Optimizing a Trainium2 kernel written in Bass

```python
import jax, jax.numpy as jnp
from jax import lax
import numpy as np

D_MODEL = 1024
BATCH = 4
SEQ = 4096
DEPTH = 4
DEC_BATCH = 2
DEC_SEQ = 8192
PAST_LEN = 128

D_FF = 2816
NORM_EPS = 1e-6
ROPE_THETA = 10000.0
BLOCK = 128

MLA_HEADS = 8
MLA_Q_RANK = 256
MLA_KV_RANK = 128
MLA_NOPE = 64
MLA_ROPE = 32
MLA_V = 64

CONV_WIDTH = 512
CONV_K = 3

SWA_HEADS = 8
SWA_KV_HEADS = 2
SWA_HEAD_DIM = 64
SWA_WINDOW = 128

N_BRANCHES = 3

MLA_COLS = MLA_Q_RANK + MLA_KV_RANK + MLA_ROPE
CONV_COLS = 3 * CONV_WIDTH
SWA_Q_COLS = SWA_HEADS * SWA_HEAD_DIM
SWA_KV_COLS = SWA_KV_HEADS * SWA_HEAD_DIM
SWA_COLS = SWA_Q_COLS + 2 * SWA_KV_COLS
GATE_COLS = N_BRANCHES * D_MODEL
IN_COLS = MLA_COLS + CONV_COLS + SWA_COLS + GATE_COLS

kernel_name = "hybrid_mla_conv_swa_encoder"


def rmsnorm(x, g):
    xf = x.astype(jnp.float32)
    y = xf * lax.rsqrt(jnp.mean(xf * xf, axis=-1, keepdims=True) + NORM_EPS)
    return (y * g.astype(jnp.float32)).astype(x.dtype)


def rope_tables(seq, dim):
    inv = 1.0 / (ROPE_THETA ** (jnp.arange(0, dim, 2, dtype=jnp.float32) / dim))
    ang = jnp.arange(seq, dtype=jnp.float32)[:, None] * inv[None, :]
    return jnp.cos(ang), jnp.sin(ang)


def apply_rope(x, cos, sin):
    xf = x.astype(jnp.float32)
    x1, x2 = jnp.split(xf, 2, axis=-1)
    out = jnp.concatenate([x1 * cos - x2 * sin, x2 * cos + x1 * sin], axis=-1)
    return out.astype(x.dtype)


def swiglu_ffn(x, g, w_gu, w_down):
    h = rmsnorm(x, g) @ w_gu
    a, b = jnp.split(h, 2, axis=-1)
    return (jax.nn.silu(a) * b) @ w_down


def mla_mixer(q_lat, kv_lat, k_rope_raw, q_norm, w_uq, kv_norm, w_ukv, w_out, cos, sin):
    B, S, _ = q_lat.shape
    nb = S // BLOCK
    q = (rmsnorm(q_lat, q_norm) @ w_uq).reshape(B, S, MLA_HEADS, MLA_NOPE + MLA_ROPE)
    q_nope = q[..., :MLA_NOPE]
    q_rope = apply_rope(q[..., MLA_NOPE:], cos[:, None, :], sin[:, None, :])
    kv = (rmsnorm(kv_lat, kv_norm) @ w_ukv).reshape(B, S, MLA_HEADS, MLA_NOPE + MLA_V)
    k_nope = kv[..., :MLA_NOPE]
    v = kv[..., MLA_NOPE:]
    k_rope = apply_rope(k_rope_raw, cos, sin)
    scale = (MLA_NOPE + MLA_ROPE) ** -0.5
    qn = q_nope.reshape(B, nb, BLOCK, MLA_HEADS, MLA_NOPE).transpose(1, 0, 2, 3, 4)
    qr = q_rope.reshape(B, nb, BLOCK, MLA_HEADS, MLA_ROPE).transpose(1, 0, 2, 3, 4)

    def attend(qb):
        qn_b, qr_b = qb
        s = (jnp.einsum('bqhd,bkhd->bhqk', qn_b, k_nope)
             + jnp.einsum('bqhr,bkr->bhqk', qr_b, k_rope)).astype(jnp.float32) * scale
        p = jax.nn.softmax(s, axis=-1).astype(v.dtype)
        return jnp.einsum('bhqk,bkhd->bqhd', p, v)

    o = lax.map(attend, (qn, qr))
    o = o.transpose(1, 0, 2, 3, 4).reshape(B, S, MLA_HEADS * MLA_V)
    return o @ w_out


def short_conv_mixer(cols, conv_w, w_out):
    S = cols.shape[1]
    b_gate, c_gate, x_in = jnp.split(cols, 3, axis=-1)
    z = c_gate * x_in
    half = CONV_K // 2
    zp = jnp.pad(z, ((0, 0), (half, half), (0, 0)))
    y = sum(zp[:, k:k + S, :] * conv_w[k] for k in range(CONV_K))
    return (b_gate * y) @ w_out


def swa_mixer(q, k, v, sink, w_out, cos, sin):
    B, S = q.shape[0], q.shape[1]
    nb = S // BLOCK
    G = SWA_HEADS // SWA_KV_HEADS
    q = apply_rope(q.reshape(B, S, SWA_HEADS, SWA_HEAD_DIM), cos[:, None, :], sin[:, None, :])
    k = apply_rope(k.reshape(B, S, SWA_KV_HEADS, SWA_HEAD_DIM), cos[:, None, :], sin[:, None, :])
    v = v.reshape(B, S, SWA_KV_HEADS, SWA_HEAD_DIM)
    qb = q.reshape(B, nb, BLOCK, SWA_KV_HEADS, G, SWA_HEAD_DIM)

    def band(t):
        tp = jnp.pad(t, ((0, 0), (BLOCK, BLOCK), (0, 0), (0, 0)))
        tp = tp.reshape(B, nb + 2, BLOCK, SWA_KV_HEADS, SWA_HEAD_DIM)
        return jnp.concatenate([tp[:, :-2], tp[:, 1:-1], tp[:, 2:]], axis=2)

    kw, vw = band(k), band(v)
    a = jnp.arange(BLOCK)[:, None]
    c = jnp.arange(3 * BLOCK)[None, :]
    rel = c - BLOCK - a
    j = jnp.arange(nb)[:, None, None] * BLOCK - BLOCK + c[None]
    mask = (jnp.abs(rel) <= SWA_WINDOW)[None] & (j >= 0) & (j < S)
    scale = SWA_HEAD_DIM ** -0.5
    s = jnp.einsum('bnqhgd,bnkhd->bnhgqk', qb, kw).astype(jnp.float32) * scale
    s = jnp.where(mask[None, :, None, None], s, -1e30)
    sk = sink.astype(jnp.float32).reshape(SWA_KV_HEADS, G)[None, None, :, :, None, None]
    m = jnp.maximum(jnp.max(s, axis=-1, keepdims=True), sk)
    p = jnp.exp(s - m)
    denom = jnp.sum(p, axis=-1, keepdims=True) + jnp.exp(sk - m)
    p = (p / denom).astype(v.dtype)
    o = jnp.einsum('bnhgqk,bnkhd->bnqhgd', p, vw).reshape(B, S, SWA_HEADS * SWA_HEAD_DIM)
    return o @ w_out


def encoder_layer(x, p, l, mla_cs, swa_cs):
    B, S, D = x.shape
    x = x + 0.5 * swiglu_ffn(x, p['ffn1_norm'][l], p['ffn1_w_gu'][l], p['ffn1_w_down'][l])
    u = rmsnorm(x, p['mix_norm'][l])
    cols = u @ p['w_in'][l]
    o0 = 0
    q_lat = cols[..., o0:o0 + MLA_Q_RANK]; o0 += MLA_Q_RANK
    kv_lat = cols[..., o0:o0 + MLA_KV_RANK]; o0 += MLA_KV_RANK
    k_rope_raw = cols[..., o0:o0 + MLA_ROPE]; o0 += MLA_ROPE
    conv_cols = cols[..., o0:o0 + CONV_COLS]; o0 += CONV_COLS
    sq = cols[..., o0:o0 + SWA_Q_COLS]; o0 += SWA_Q_COLS
    sk = cols[..., o0:o0 + SWA_KV_COLS]; o0 += SWA_KV_COLS
    sv = cols[..., o0:o0 + SWA_KV_COLS]; o0 += SWA_KV_COLS
    gate_cols = cols[..., o0:o0 + GATE_COLS]

    y_a = mla_mixer(q_lat, kv_lat, k_rope_raw, p['mla_q_norm'][l], p['mla_w_uq'][l],
                    p['mla_kv_norm'][l], p['mla_w_ukv'][l], p['mla_w_o'][l], *mla_cs)
    y_b = short_conv_mixer(conv_cols, p['conv_w'][l], p['conv_w_o'][l])
    y_c = swa_mixer(sq, sk, sv, p['swa_sink'][l], p['swa_w_o'][l], *swa_cs)

    gates = jax.nn.sigmoid(gate_cols.astype(jnp.float32)).astype(x.dtype).reshape(B, S, N_BRANCHES, D)
    merged = gates[:, :, 0] * y_a + gates[:, :, 1] * y_b + gates[:, :, 2] * y_c
    x = x + merged @ p['w_o'][l]
    x = x + 0.5 * swiglu_ffn(x, p['ffn2_norm'][l], p['ffn2_w_gu'][l], p['ffn2_w_down'][l])
    return x


def encode(x, p, final_norm):
    S = x.shape[1]
    mla_cs = rope_tables(S, MLA_ROPE)
    swa_cs = rope_tables(S, SWA_HEAD_DIM)
    for l in range(DEPTH):
        x = encoder_layer(x, p, l, mla_cs, swa_cs)
    return rmsnorm(x, final_norm)


def setup_inputs(seed: int = 0) -> dict:
    key = jax.random.key(seed)
    ks = jax.random.split(key, 24)

    def w(k, shape, fan_in):
        return jax.random.normal(k, shape, jnp.float32) * (fan_in ** -0.5)

    def gain(k, shape):
        return 1.0 + 0.02 * jax.random.normal(k, shape, jnp.float32)

    L = DEPTH
    return {
        "x_prompt": jax.random.normal(ks[0], (BATCH, SEQ, D_MODEL), jnp.float32),
        "x_sample": jax.random.normal(ks[1], (DEC_BATCH, DEC_SEQ, D_MODEL), jnp.float32),
        "ffn1_norm": gain(ks[2], (L, D_MODEL)),
        "ffn1_w_gu": w(ks[3], (L, D_MODEL, 2 * D_FF), D_MODEL),
        "ffn1_w_down": w(ks[4], (L, D_FF, D_MODEL), D_FF),
        "mix_norm": gain(ks[5], (L, D_MODEL)),
        "w_in": w(ks[6], (L, D_MODEL, IN_COLS), D_MODEL),
        "mla_q_norm": gain(ks[7], (L, MLA_Q_RANK)),
        "mla_w_uq": w(ks[8], (L, MLA_Q_RANK, MLA_HEADS * (MLA_NOPE + MLA_ROPE)), MLA_Q_RANK),
        "mla_kv_norm": gain(ks[9], (L, MLA_KV_RANK)),
        "mla_w_ukv": w(ks[10], (L, MLA_KV_RANK, MLA_HEADS * (MLA_NOPE + MLA_V)), MLA_KV_RANK),
        "mla_w_o": w(ks[11], (L, MLA_HEADS * MLA_V, D_MODEL), MLA_HEADS * MLA_V),
        "conv_w": w(ks[12], (L, CONV_K, CONV_WIDTH), CONV_K),
        "conv_w_o": w(ks[13], (L, CONV_WIDTH, D_MODEL), CONV_WIDTH),
        "swa_sink": 0.5 * jax.random.normal(ks[14], (L, SWA_HEADS), jnp.float32),
        "swa_w_o": w(ks[15], (L, SWA_HEADS * SWA_HEAD_DIM, D_MODEL), SWA_HEADS * SWA_HEAD_DIM),
        "w_o": w(ks[16], (L, D_MODEL, D_MODEL), D_MODEL),
        "ffn2_norm": gain(ks[17], (L, D_MODEL)),
        "ffn2_w_gu": w(ks[18], (L, D_MODEL, 2 * D_FF), D_MODEL),
        "ffn2_w_down": w(ks[19], (L, D_FF, D_MODEL), D_FF),
        "final_norm": gain(ks[20], (D_MODEL,)),
    }


def reference(x_prompt, x_sample, ffn1_norm, ffn1_w_gu, ffn1_w_down, mix_norm, w_in,
              mla_q_norm, mla_w_uq, mla_kv_norm, mla_w_ukv, mla_w_o, conv_w, conv_w_o,
              swa_sink, swa_w_o, w_o, ffn2_norm, ffn2_w_gu, ffn2_w_down, final_norm):
    p = {
        'ffn1_norm': ffn1_norm, 'ffn1_w_gu': ffn1_w_gu, 'ffn1_w_down': ffn1_w_down,
        'mix_norm': mix_norm, 'w_in': w_in,
        'mla_q_norm': mla_q_norm, 'mla_w_uq': mla_w_uq, 'mla_kv_norm': mla_kv_norm,
        'mla_w_ukv': mla_w_ukv, 'mla_w_o': mla_w_o,
        'conv_w': conv_w, 'conv_w_o': conv_w_o,
        'swa_sink': swa_sink, 'swa_w_o': swa_w_o, 'w_o': w_o,
        'ffn2_norm': ffn2_norm, 'ffn2_w_gu': ffn2_w_gu, 'ffn2_w_down': ffn2_w_down,
    }
    y_prompt = encode(x_prompt, p, final_norm)
    y_sample = encode(x_sample, p, final_norm)
    return (y_prompt, y_sample)
```

```python
import contextlib
import numpy as np
import ml_dtypes
import concourse.bass as bass
import concourse.mybir as mybir
from concourse.bass_utils import run_bass_kernel_spmd

F32 = mybir.dt.float32
BF16 = mybir.dt.bfloat16
AF = mybir.ActivationFunctionType
ALU = mybir.AluOpType

D = 1024
DFF = 2816
L = 4
T = 4096
TT = 512
NT = T // TT
NC8 = D // 128
NJ = DFF // 128
EPS = 1e-6


class Res:
    __slots__ = ("name", "w", "r")

    def __init__(self, name):
        self.name = name
        self.w = None
        self.r = {}


class Op:
    __slots__ = ("eng", "fn", "deps", "signaled", "dma", "sem", "val", "pre", "idx", "cc")

    def __init__(self, eng, fn, dma, cc=False):
        self.eng = eng
        self.fn = fn
        self.dma = dma or cc
        self.cc = cc
        self.deps = []
        self.signaled = dma or cc
        self.sem = None
        self.val = 0
        self.pre = None
        self.idx = 0


ENGS = ("pe", "act", "dve", "pool", "sp")
NDMA = {"sp": 12, "pool": 8, "act": 4}


class Sched:
    def __init__(self):
        self.ops = {e: [] for e in ENGS}
        self.last = {e: None for e in ENGS}
        self.dmas = {q: [] for q in NDMA}
        self.pending_barrier = {e: None for e in ENGS}
        self.ccs = []
        self.n = 0

    def add(self, eng, fn, reads=(), writes=(), dma=False, cc=False):
        op = Op(eng, fn, dma, cc)
        dma = dma or cc
        op.idx = self.n
        self.n += 1
        deps = {}

        def dep(o, war=False, waw=False):
            if o is None or o is op:
                return
            if (not o.dma) and (not dma) and o.eng == eng:
                if eng == "pe" or war or waw:
                    return
            deps[id(o)] = o

        pb = self.pending_barrier[eng]
        if pb is not None:
            for o in pb:
                dep(o)
            self.pending_barrier[eng] = None
        for r in reads:
            dep(r.w)
        rset = set(id(r) for r in reads)
        for r in writes:
            dep(r.w, waw=(id(r) not in rset))
            for o in r.r.values():
                dep(o, war=True)
        key = ("dma", op.idx) if dma else eng
        for r in reads:
            r.r[key] = op
        for r in writes:
            r.w = op
            r.r = {}
        op.deps = list(deps.values())
        for o in op.deps:
            o.signaled = True
        self.ops[eng].append(op)
        if cc:
            self.ccs.append(op)
        elif dma:
            self.dmas[eng].append(op)
        else:
            self.last[eng] = op
        return op

    def barrier(self):
        outstanding = []
        for e in ENGS:
            if self.last[e] is not None:
                outstanding.append(self.last[e])
        for q, lst in self.dmas.items():
            outstanding.extend(lst[-NDMA[q]:])
        outstanding.extend(self.ccs[-3:])
        for e in ENGS:
            self.pending_barrier[e] = list(outstanding)

    def emit(self, nc, es):
        sems = {e: es.enter_context(nc.semaphore("sem_" + e)) for e in ("pe", "act", "dve", "pool")}
        dsems = {q: [es.enter_context(nc.semaphore("dsem_%s%d" % (q, i))) for i in range(n)]
                 for q, n in NDMA.items()}
        for e in ENGS:
            cnt = 0
            di = 0
            for op in self.ops[e]:
                if op.cc:
                    op.sem = es.enter_context(nc.semaphore("ccsem%d" % op.idx))
                    op.val = 1
                elif op.dma:
                    n = NDMA[e]
                    op.sem = dsems[e][di % n]
                    op.val = 16 * (di // n + 1)
                    if di >= n:
                        op.pre = (op.sem, 16 * (di // n))
                    di += 1
                elif op.signaled:
                    cnt += 1
                    op.sem = sems[e]
                    op.val = cnt
        block = es.enter_context(nc.Block())
        handles = {"pe": block.tensor, "act": block.scalar, "dve": block.vector,
                   "pool": block.gpsimd, "sp": block.sync}

        def make(e):
            def body(eng):
                waited = {}

                def wait(sem, val):
                    k = id(sem)
                    if waited.get(k, 0) < val:
                        eng.wait_ge(sem, val)
                        waited[k] = val

                for op in self.ops[e]:
                    need = {}
                    for d in op.deps:
                        k = id(d.sem)
                        if k not in need or need[k][1] < d.val:
                            need[k] = (d.sem, d.val)
                    if op.pre is not None:
                        k = id(op.pre[0])
                        if k not in need or need[k][1] < op.pre[1]:
                            need[k] = op.pre
                    for sem, val in need.values():
                        wait(sem, val)
                    ins = op.fn(eng)
                    if op.cc:
                        ins.then_inc(op.sem)
                    elif op.dma:
                        ins.then_inc(op.sem, 16)
                    elif op.signaled:
                        ins.then_inc(op.sem, 1)
                if e in NDMA:
                    for op in self.dmas[e][-NDMA[e]:]:
                        wait(op.sem, op.val)
            return body

        for e in ENGS:
            handles[e](make(e))


QL0, KV0, KR0, CB0, CC0, CX0, SQ0, SK0, SV0, GA0, GB0, GC0 = 0, 256, 384, 416, 928, 1440, 1952, 2464, 2592, 2720, 3744, 4768
SWAP32 = np.concatenate([np.arange(16, 32), np.arange(0, 16)])
SWAP64 = np.concatenate([np.arange(32, 64), np.arange(0, 32)])
SWAP64X2 = np.concatenate([SWAP64, 64 + SWAP64])
AR = np.arange
SLABS_A = [
    QL0 + AR(256),
    np.concatenate([KV0 + AR(128), KR0 + AR(32), KR0 + SWAP32, KR0 + AR(32), KR0 + AR(32)]),
    CC0 + AR(256), CC0 + 256 + AR(256), CX0 + AR(256), CX0 + 256 + AR(256),
    np.concatenate([SK0 + AR(128), SK0 + SWAP64X2]),
    np.concatenate([SV0 + AR(128), SV0 + AR(128)]),
]
SLABS_B = [CB0 + AR(256), CB0 + 256 + AR(256)] + [
    np.concatenate([SQ0 + i * 128 + AR(128), SQ0 + i * 128 + SWAP64X2]) for i in range(4)]
SLABS_G = [np.concatenate([GA0 + m * 128 + AR(128), GB0 + m * 128 + AR(128), GC0 + m * 128 + AR(128)]) for m in range(8)]


def lay_gu(w):
    Lw = w.shape[0]
    a = w[:, :, :DFF].reshape(Lw, 8, 128, NJ, 128)
    b = w[:, :, DFF:].reshape(Lw, 8, 128, NJ, 128)
    ab = np.stack([a, b], axis=4)
    return np.ascontiguousarray(ab.transpose(0, 3, 2, 1, 4, 5)).reshape(Lw, NJ, 128, 8 * 256)


def lay_down(w):
    Lw = w.shape[0]
    x = w.reshape(Lw, NJ, 128, 8, 128)
    return np.ascontiguousarray(x.transpose(0, 3, 2, 1, 4)).reshape(Lw, 8, 128, NJ * 128)


def lay_colslabs(w, slabs):
    Lw = w.shape[0]
    out = []
    for cols in slabs:
        x = w[:, :, cols].reshape(Lw, 8, 128, len(cols)).transpose(0, 2, 1, 3)
        out.append(x.reshape(Lw, 128, 8 * len(cols)))
    return np.ascontiguousarray(np.stack(out, axis=1))


def lay_rows(w, nk):
    Lw, C = w.shape[0], w.shape[2]
    return w.reshape(Lw, nk, 128, C).transpose(0, 2, 1, 3)


def lay_vec(v):
    Lw, n = v.shape[0], v.shape[1] // 128
    return np.ascontiguousarray(v.reshape(Lw, n, 128).transpose(2, 0, 1)).reshape(128, Lw * n)


G_F1, G_MIX, G_F2, G_FIN, G_Q, G_KV, G_CW, NV = 0, 32, 64, 96, 104, 112, 116, 164


def rope_tab(pos, dim):
    inv = (1.0 / (np.float32(10000.0) ** (np.arange(0, dim, 2, dtype=np.float32) / np.float32(dim)))).astype(np.float32)
    ang = (pos.astype(np.float32)[:, None] * inv[None, :]).astype(np.float32)
    return np.cos(ang).astype(np.float32).T, np.sin(ang).astype(np.float32).T


def prep_inputs(inp):
    f = lambda k: np.asarray(inp[k], dtype=np.float32)
    vecs = np.zeros((128, NV), np.float32)
    vecs[:, G_F1:G_F1 + 32] = lay_vec(f("ffn1_norm"))
    vecs[:, G_MIX:G_MIX + 32] = lay_vec(f("mix_norm"))
    vecs[:, G_F2:G_F2 + 32] = lay_vec(f("ffn2_norm"))
    vecs[:, G_FIN:G_FIN + 8] = lay_vec(f("final_norm")[None, :])
    vecs[:, G_Q:G_Q + 8] = lay_vec(f("mla_q_norm"))
    vecs[:, G_KV:G_KV + 4] = lay_vec(f("mla_kv_norm"))
    cw = f("conv_w").reshape(L, 3, 4, 128).transpose(3, 0, 2, 1)
    vecs[:, G_CW:G_CW + 48] = cw.reshape(128, 48)
    w_in = f("w_in")
    wuq = f("mla_w_uq")
    uq = np.zeros((L, 128, 2, 8, 128), np.float32)
    for h in range(8):
        cols = np.concatenate([h * 96 + AR(64), h * 96 + 64 + AR(32), h * 96 + 64 + SWAP32])
        uq[:, :, :, h, :] = wuq[:, :, cols].reshape(L, 2, 128, 128).transpose(0, 2, 1, 3)
    wy = np.concatenate([lay_rows(f("mla_w_o"), 4), lay_rows(f("conv_w_o"), 4), lay_rows(f("swa_w_o"), 4)], axis=2)
    wy = wy.reshape(L, 128, 12, 8, 128).transpose(0, 3, 1, 2, 4).reshape(L, 8, 128, 12 * 128)
    wo = lay_rows(f("w_o"), 8).reshape(L, 128, 8, 8, 128).transpose(0, 3, 1, 2, 4).reshape(L, 8, 128, 8 * 128)
    masks = np.zeros((128, 2, 128), np.float32)
    cc, aa = np.meshgrid(np.arange(128), np.arange(128), indexing="ij")
    masks[:, 0, :] = (cc >= aa)
    masks[:, 1, :] = (cc <= aa)
    sel01 = np.zeros((8, 128), np.float32)
    sel01[:, 64:] = 1.0
    rsel = np.zeros((8, 4, 256), np.float32)
    for g in range(2):
        for par in range(2):
            for j in range(2):
                rsel[4 * g + par + 2 * j, g * 2 + par, j * 128:(j + 1) * 128] = 1.0
    shared = {
        "wgu1": lay_gu(f("ffn1_w_gu")), "wdn1": lay_down(f("ffn1_w_down")),
        "wgu2": lay_gu(f("ffn2_w_gu")), "wdn2": lay_down(f("ffn2_w_down")),
        "winA": lay_colslabs(w_in, SLABS_A), "winB": lay_colslabs(w_in, SLABS_B), "winG": lay_colslabs(w_in, SLABS_G),
        "wuq": np.ascontiguousarray(uq.reshape(L, 128, 2 * 8 * 128)),
        "wukv": np.ascontiguousarray(f("mla_w_ukv")),
        "wy": np.ascontiguousarray(wy), "wo": np.ascontiguousarray(wo),
        "vecs": vecs, "sinkT": np.ascontiguousarray(f("swa_sink").T),
        "masks": masks, "sel01": sel01, "rsel": rsel,
    }
    xp, xs = f("x_prompt"), f("x_sample")
    per_core = []
    for c in range(8):
        rank = c % 2
        cvec = np.zeros((128, 66), np.float32)
        if c < 4:
            xc = xp[c]
            off = 0
            cvec[:, rank * 32:(rank + 1) * 32] = 1.0
        else:
            i = c - 4
            xc = xs[i // 2, rank * T:(rank + 1) * T]
            off = rank * T
            cvec[:, 0:64] = 1.0
            cvec[:, 64] = 1.0 if rank == 1 else 0.0
            cvec[:, 65] = 1.0 if rank == 0 else 0.0
        pos = np.arange(off, off + T)
        c32, s32 = rope_tab(pos, 32)
        c64, s64 = rope_tab(pos, 64)
        C32 = np.concatenate([c32, c32], 0)
        S32 = np.concatenate([-s32, s32], 0)
        C64 = np.concatenate([c64, c64], 0)
        S64 = np.concatenate([-s64, s64], 0)
        d = dict(shared)
        d["xT"] = np.ascontiguousarray(xc.T)
        d["cvec"] = cvec
        d["ropeM"] = np.ascontiguousarray(np.concatenate([C32, S32, C32, S32], 0))
        d["ropeSC"] = np.ascontiguousarray(np.concatenate([C64, C64], 0))
        d["ropeSS"] = np.ascontiguousarray(np.concatenate([S64, S64], 0))
        per_core.append(d)
    return per_core


class Cfg:
    layers = L
    ffn1 = True
    mixer = True
    ffn2 = True
    final = True
    debug = False
    mla = True
    p2a = True
    p2b = True


STQ = "pool"


class Stream:
    def __init__(self, S, bufs, bres, loads, look):
        assert look <= len(bufs) - 1
        self.S, self.bufs, self.bres, self.loads, self.look = S, bufs, bres, loads, look
        self.nxt = 0

    def get(self, i):
        while self.nxt <= min(i + self.look, len(self.loads) - 1):
            k = self.nxt
            s = k % len(self.bufs)
            eng, mk, reads = self.loads[k]
            self.S.add(eng, mk(self.bufs[s]), reads=reads, writes=[self.bres[s]], dma=True)
            self.nxt += 1
        s = i % len(self.bufs)
        return self.bufs[s], self.bres[s]


def build(cfg):
    nc = bass.Bass("TRN2", target_bir_lowering=False)
    S = Sched()
    es = contextlib.ExitStack()
    with es:
        def dram_in(name, shape, dt=F32):
            return nc.dram_tensor(name, list(shape), dt, kind="ExternalInput").ap()

        def dram_tmp(name, shape, dt):
            return nc.dram_tensor(name, list(shape), dt).ap()

        xT = dram_in("xT", [D, T])
        yT = nc.dram_tensor("yT", [D, T], F32, kind="ExternalOutput").ap()
        dbg = nc.dram_tensor("dbg", [128, 64 * TT], F32, kind="ExternalOutput").ap() if cfg.debug else None
        WSPEC = {"gu1": ("wgu1", [L, NJ, 128, 2048]), "dn1": ("wdn1", [L, 8, 128, NJ * 128]),
                 "gu2": ("wgu2", [L, NJ, 128, 2048]), "dn2": ("wdn2", [L, 8, 128, NJ * 128]),
                 "inA": ("winA", [L, 8, 128, 2048]), "inB": ("winB", [L, 6, 128, 2048]),
                 "inG": ("winG", [L, 8, 128, 3072]), "wy": ("wy", [L, 8, 128, 1536]), "wo": ("wo", [L, 8, 128, 1024])}
        WF = {k: dram_in(v[0], v[1]) for k, v in WSPEC.items()}
        WB = {k: dram_tmp(v[0] + "_bf", v[1], BF16) for k, v in WSPEC.items()}
        wuq_d = dram_in("wuq", [L, 128, 2048])
        wukv_d = dram_in("wukv", [L, 128, 1024])
        wukv_b = dram_tmp("wukv_bf", [L, 128, 1024], BF16)
        ukvres = {}
        vecs_d = dram_in("vecs", [128, NV])
        sink_d = dram_in("sinkT", [8, L])
        masks_d = dram_in("masks", [128, 2, 128])
        sel01_d = dram_in("sel01", [8, 128])
        rsel_d = dram_in("rsel", [8, 4, 256])
        cvec_d = dram_in("cvec", [128, 66])
        ropeM_d = dram_in("ropeM", [128, T])
        ropeSC_d = dram_in("ropeSC", [128, T])
        ropeSS_d = dram_in("ropeSS", [128, T])
        Qs = dram_tmp("Qs", [8, 96, T], BF16)
        ag1s = nc.dram_tensor("ag1s", [128, T], BF16)
        ag1o = nc.dram_tensor("ag1o", [256, T], BF16)
        ag2s = nc.dram_tensor("ag2s", [64, T], BF16)
        ag2o = nc.dram_tensor("ag2o", [128, T], BF16)
        ag3s = nc.dram_tensor("ag3s", [128, 8], F32)
        ag3o = nc.dram_tensor("ag3o", [256, 8], F32)
        Ksc = dram_tmp("Ksc", [128, T + 256], BF16)
        Vsc = dram_tmp("Vsc", [T + 256, 256], BF16)
        zs = dram_tmp("zs", [4, 128, T + 2], F32)
        Os = dram_tmp("Os", [512, T], BF16)
        BYs = dram_tmp("BYs", [512, T], BF16)
        OSWs = dram_tmp("OSWs", [512, T], BF16)
        R = {n: Res(n) for n in ("Qs", "ag1s", "ag1o", "ag2s", "ag2o", "ag3s", "ag3o", "Ksc", "Vsc", "zs", "Os", "BYs", "OSWs", "Ksc_h", "Vsc_h", "zs_h")}

        uid = [0]

        def sb(name, shape, dt, stack=es):
            uid[0] += 1
            return stack.enter_context(nc.sbuf_tensor("s%d_%s" % (uid[0], name), list(shape), dt))

        X = sb("X", [128, NC8, T], F32)
        Xr = [[Res("X%d_%d" % (c, t)) for t in range(NT)] for c in range(NC8)]
        G = sb("G", [128, NV], F32)
        CV = sb("CV", [128, 66], F32)
        ones = sb("ones", [128, 128], BF16)
        epsb = sb("epsb", [128, 1], F32)
        masks = sb("masksb", [128, 2, 128], BF16)
        sel01 = sb("sel01b", [8, 128], F32)
        rsel = sb("rselb", [8, 4, 256], BF16)
        sinkT = sb("sinkTb", [8, L], F32)
        Cr = Res("consts")
        PS = es.enter_context(nc.psum_tensor("PS", [128, 8, TT], F32))
        banks = [PS[:, i, :] for i in range(8)]
        bankr = [Res("bank%d" % i) for i in range(8)]
        bstate = [0]

        def nbank():
            i = bstate[0] % 8
            bstate[0] += 1
            return PS[:, i, :], bankr[i]

        S.add("pool", lambda e: e.memset(ones[:], 1.0), writes=[Cr])
        S.add("pool", lambda e: e.memset(epsb[:], EPS), writes=[Cr])
        S.add("sp", lambda e: e.dma_start(out=G[:], in_=vecs_d), writes=[Cr], dma=True)
        S.add("sp", lambda e: e.dma_start(out=CV[:], in_=cvec_d), writes=[Cr], dma=True)
        S.add("sp", lambda e: e.dma_start(out=sel01[:], in_=sel01_d), writes=[Cr], dma=True)
        S.add("sp", lambda e: e.dma_start(out=sinkT[:], in_=sink_d), writes=[Cr], dma=True)
        S.add("pool", lambda e: e.dma_start(out=masks[:], in_=masks_d), writes=[Cr], dma=True)
        S.add("pool", lambda e: e.dma_start(out=rsel[:], in_=rsel_d), writes=[Cr], dma=True)
        for t in range(NT):
            S.add("sp", lambda e, t=t: e.dma_start(
                out=X[:, :, t * TT:(t + 1) * TT],
                in_=xT.rearrange("(c p) t -> p c t", p=128)[:, :, t * TT:(t + 1) * TT]),
                writes=[Xr[c][t] for c in range(NC8)], dma=True)
        for hh in range(2):
            S.add("sp", lambda e, hh=hh: e.dma_start(
                out=ag2s[56:64, :].rearrange("a (b c) -> (a b) c", c=256)[:, hh * 128:(hh + 1) * 128], in_=ones[:]),
                reads=[Cr], writes=[R["ag2s"]], dma=True)
        S.add("act", lambda e: e.activation(out=sinkT[:], in_=sinkT[:], func=AF.Exp), reads=[Cr], writes=[Cr])

        wres = {}

        def cast_w(nm, l):
            src, dst = WF[nm], WB[nm]
            r = Res("w%s_%d" % (nm, l))
            wres[(nm, l)] = r
            nsp = src.shape[1]
            half = nsp // 2
            for (a, b) in ((0, half), (half, nsp)):
                S.add("pool", lambda e, a=a, b=b: e.dma_start(
                    out=dst[l, a:b].rearrange("s p c -> (s p) c"),
                    in_=src[l, a:b].rearrange("s p c -> (s p) c"), max_dma_last_dim=4096),
                    writes=[r], dma=True)

        def dump(slot, ap, reads, n=1):
            if dbg is None:
                return
            S.add("pool", lambda e: e.dma_start(out=dbg[0:ap.shape[0], slot * TT:(slot + n) * TT], in_=ap), reads=reads, dma=True)

        def norm_stat(srcs, reads, nfeat, rs, rsr, P):
            bk, bkr = nbank()
            n = len(srcs)
            for c in range(n):
                q, qr = P["sq"][c % 2], P["sqr"][c % 2]
                S.add("act", lambda e, c=c, q=q: e.activation(out=q[:], in_=srcs[c], func=AF.Square),
                      reads=[reads[c]], writes=[qr])
                S.add("pe", lambda e, c=c, q=q: e.matmul(bk, lhsT=ones[:], rhs=q[:], start=(c == 0), stop=(c == n - 1)),
                      reads=[qr, Cr], writes=[bkr])
            S.add("act", lambda e: e.activation(out=rs[:], in_=bk, func=AF.Ln, bias=epsb[:], scale=1.0 / nfeat),
                  reads=[bkr, Cr], writes=[rsr])
            S.add("act", lambda e: e.activation(out=rs[:], in_=rs[:], func=AF.Exp, scale=-0.5), reads=[rsr], writes=[rsr])

        def rmsnorm_stat(t, P, rs, rsr):
            cols = slice(t * TT, (t + 1) * TT)
            norm_stat([X[:, c, cols] for c in range(NC8)], [Xr[c][t] for c in range(NC8)], D, rs, rsr, P)

        def rmsnorm_apply(t, gcol, P, rs, rsr):
            cols = slice(t * TT, (t + 1) * TT)
            for c in range(NC8):
                S.add("dve", lambda e, c=c: e.scalar_tensor_tensor(
                    out=P["xn"][:, c, :], in0=X[:, c, cols], scalar=G[:, gcol + c:gcol + c + 1], in1=rs[:],
                    op0=ALU.mult, op1=ALU.mult),
                    reads=[Xr[c][t], Cr, rsr], writes=[P["xnr"][c]])

        def rmsnorm_tile(t, gcol, P):
            rmsnorm_stat(t, P, P["rstd"], P["rstdr"])
            rmsnorm_apply(t, gcol, P, P["rstd"], P["rstdr"])

        PG = {}
        PG["xn"] = sb("xn", [128, NC8, TT], BF16)
        PG["xnr"] = [Res("xn%d" % c) for c in range(NC8)]
        pre_done = [None]

        def norm_bufs(ps, P):
            P.update(PG)
            P["sq"] = [sb("sq%d" % i, [128, TT], BF16, ps) for i in range(2)]
            P["sqr"] = [Res("sq%d" % i) for i in range(2)]
            P["rstd"] = sb("rstd", [128, TT], F32, ps)
            P["rstdr"] = Res("rstd")

        def first_norm(t, gcol, P):
            if pre_done[0] == (t, gcol):
                pre_done[0] = None
                return
            assert pre_done[0] is None, pre_done[0]
            rmsnorm_tile(t, gcol, P)

        def pre_norm(nxt, P):
            if nxt is not None:
                rmsnorm_tile(nxt[0], nxt[1], P)
                pre_done[0] = nxt

        def slab_stream(ps, name, nm, l, nslab_per_tile, shape, nbuf, look, ntiles=NT):
            bufs = [sb("%s%d" % (name, i), shape, BF16, ps) for i in range(nbuf)]
            bres = [Res("%s%d" % (name, i)) for i in range(nbuf)]
            loads = []
            k = shape[1]
            for t in range(ntiles):
                for j in range(nslab_per_tile):
                    loads.append(("sp", (lambda buf, j=j: (lambda e: e.dma_start(
                        out=buf[:], in_=WB[nm][l, j].rearrange("p (k c) -> p k c", k=k)))), [wres[(nm, l)]]))
            return Stream(S, bufs, bres, loads, look)

        def ffn_phase(l, which, nxt=None):
            nm_gu, nm_dn = ("gu1", "dn1") if which == 1 else ("gu2", "dn2")
            gcol = (G_F1 if which == 1 else G_F2) + l * 8
            with contextlib.ExitStack() as ps:
                P = {}
                norm_bufs(ps, P)
                hid = sb("hid", [128, NJ, TT], BF16, ps)
                hidr = [Res("hid%d" % j) for j in range(NJ)]
                NSA = 3
                sa = [sb("sa%d" % i, [128, TT], BF16, ps) for i in range(NSA)]
                sar = [Res("sa%d" % i) for i in range(NSA)]
                gst = slab_stream(ps, "gub", nm_gu, l, NJ, [128, 8, 256], 4, 3)
                dst_ = slab_stream(ps, "dnb", nm_dn, l, 8, [128, NJ, 128], 3, 2)
                gst.get(0)
                first_norm(0, gcol, P)
                for t in range(NT):
                    cols = slice(t * TT, (t + 1) * TT)
                    for j in range(NJ):
                        gb, gbr = gst.get(t * NJ + j)
                        if j == NJ - 3:
                            dst_.get(t * 8)
                        ba, bar_ = nbank()
                        bb, bbr = nbank()
                        for k in range(8):
                            S.add("pe", lambda e, gb=gb, k=k, ba=ba: e.matmul(
                                ba, lhsT=gb[:, k, 0:128], rhs=P["xn"][:, k, :], start=(k == 0), stop=(k == 7)),
                                reads=[gbr, P["xnr"][k]], writes=[bar_])
                        for k in range(8):
                            S.add("pe", lambda e, gb=gb, k=k, bb=bb: e.matmul(
                                bb, lhsT=gb[:, k, 128:256], rhs=P["xn"][:, k, :], start=(k == 0), stop=(k == 7)),
                                reads=[gbr, P["xnr"][k]], writes=[bbr])
                        si = j % NSA
                        S.add("act", lambda e, si=si, ba=ba: e.activation(out=sa[si][:], in_=ba, func=AF.Silu),
                              reads=[bar_], writes=[sar[si]])
                        S.add("dve", lambda e, si=si, bb=bb, j=j: e.tensor_tensor(
                            out=hid[:, j, :], in0=bb, in1=sa[si][:], op=ALU.mult),
                            reads=[bbr, sar[si]], writes=[hidr[j]])
                    if t + 1 < NT:
                        rmsnorm_tile(t + 1, gcol, P)
                    else:
                        pre_norm(nxt, P)
                    for m in range(8):
                        db, dbr = dst_.get(t * 8 + m)
                        bo, bor = nbank()
                        for j in range(NJ):
                            S.add("pe", lambda e, db=db, j=j, bo=bo: e.matmul(
                                bo, lhsT=db[:, j, :], rhs=hid[:, j, :], start=(j == 0), stop=(j == NJ - 1)),
                                reads=[dbr, hidr[j]], writes=[bor])
                        S.add("dve", lambda e, m=m, bo=bo, cols=cols: e.scalar_tensor_tensor(
                            out=X[:, m, cols], in0=bo, scalar=0.5, in1=X[:, m, cols],
                            op0=ALU.mult, op1=ALU.add),
                            reads=[bor, Xr[m][t]], writes=[Xr[m][t]])
                S.barrier()

        def pass1(l, nxt=None):
            with contextlib.ExitStack() as ps:
                P = {}
                norm_bufs(ps, P)
                gcol = G_MIX + l * 8
                st = slab_stream(ps, "wa", "inA", l, 8, [128, 8, 256], 3, 2)
                wuq = sb("wuq", [128, 2, 8, 128], BF16, ps)
                wuqr = Res("wuq")
                S.add("pool", lambda e: e.dma_start(out=wuq[:], in_=wuq_d[l].rearrange("p (k h c) -> p k h c", k=2, h=8)),
                      writes=[wuqr], dma=True)
                rq = sb("rq", [128, TT], F32, ps); rqr = Res("rq")
                rs2 = sb("rstd2", [128, TT], F32, ps)
                rsl = [(P["rstd"], P["rstdr"]), (rs2, Res("rstd2"))]
                qn = sb("qn", [128, 2, TT], BF16, ps); qnr = Res("qn")
                kvst = sb("kvst", [128, TT], BF16, ps); kvstr = Res("kvst")
                krst = sb("krst", [32, TT], BF16, ps); krstr = Res("krst")
                NTMP = 5
                tmp = [sb("tmp%d" % i, [128, TT], F32, ps) for i in range(NTMP)]
                tmpr = [Res("tmp%d" % i) for i in range(NTMP)]
                csb = sb("csb", [128, 4, TT], F32, ps); csbr = [Res("csb%d" % i) for i in range(4)]
                zst = [sb("zst%d" % i, [128, TT], F32, ps) for i in range(3)]
                zstr = [Res("zst%d" % i) for i in range(3)]
                kst = [sb("kst%d" % i, [128, TT], BF16, ps) for i in range(2)]
                kstr = [Res("kst%d" % i) for i in range(2)]
                vst = [sb("vst%d" % i, [128, 4, 2, 128], BF16, ps) for i in range(2)]
                vstr = [Res("vst%d" % i) for i in range(2)]
                qst = [sb("qst%d" % i, [96, TT], BF16, ps) for i in range(3)]
                qstr = [Res("qst%d" % i) for i in range(3)]
                rM = sb("rM", [128, TT], F32, ps); rMr = Res("rM")
                rC = sb("rC", [128, TT], F32, ps); rCr = Res("rC")
                rS = sb("rS", [128, TT], F32, ps); rSr = Res("rS")
                for i in range(2):
                    S.add("pool", lambda e, i=i: e.memset(vst[i][:], 1.0), writes=[vstr[i]])
                tcnt = [0]

                def ntmp():
                    i = tcnt[0] % NTMP
                    tcnt[0] += 1
                    return tmp[i], tmpr[i]

                zc = [0]
                qc = [0]
                st.get(0)
                first_norm(0, gcol, P)
                for t in range(NT):
                    cols = slice(t * TT, (t + 1) * TT)
                    st.get(t * 8)
                    S.add("sp", lambda e, cols=cols: e.dma_start(out=rM[:], in_=ropeM_d[:, cols]), writes=[rMr], dma=True)
                    S.add("sp", lambda e, cols=cols: e.dma_start(out=rC[:], in_=ropeSC_d[:, cols]), writes=[rCr], dma=True)
                    S.add("sp", lambda e, cols=cols: e.dma_start(out=rS[:], in_=ropeSS_d[:, cols]), writes=[rSr], dma=True)

                    def group(wb, wbr, c0, c1, bk, bkr, rows=None):
                        for k in range(8):
                            S.add("pe", lambda e, k=k: e.matmul(
                                bk if rows is None else bk[0:rows, :], lhsT=wb[:, k, c0:c1], rhs=P["xn"][:, k, :],
                                start=(k == 0), stop=(k == 7)), reads=[wbr, P["xnr"][k]], writes=[bkr])

                    def q_heads(hs, cols=cols):
                        for h in hs:
                            bk, bkr = nbank()
                            for k in range(2):
                                S.add("pe", lambda e, k=k, h=h, bk=bk: e.matmul(bk, lhsT=wuq[:, k, h, :], rhs=qn[:, k, :], start=(k == 0), stop=(k == 1)),
                                      reads=[wuqr, qnr], writes=[bkr])
                            qi = qc[0] % 3
                            qc[0] += 1
                            t1, t1r = ntmp()
                            t2, t2r = ntmp()
                            S.add("act", lambda e, bk=bk, qi=qi: e.activation(out=qst[qi][0:64, :], in_=bk[0:64, :], func=AF.Copy),
                                  reads=[bkr], writes=[qstr[qi]])
                            S.add("dve", lambda e, bk=bk, t1=t1: e.tensor_tensor(out=t1[64:96, :], in0=bk[64:96, :], in1=rM[64:96, :], op=ALU.mult),
                                  reads=[bkr, rMr], writes=[t1r])
                            S.add("dve", lambda e, bk=bk, t2=t2: e.tensor_tensor(out=t2[64:96, :], in0=bk[96:128, :], in1=rM[96:128, :], op=ALU.mult),
                                  reads=[bkr, rMr], writes=[t2r])
                            S.add("dve", lambda e, t1=t1, t2=t2, qi=qi: e.tensor_tensor(out=qst[qi][64:96, :], in0=t1[64:96, :], in1=t2[64:96, :], op=ALU.add),
                                  reads=[t1r, t2r, qstr[qi]], writes=[qstr[qi]])
                            S.add(STQ, lambda e, h=h, qi=qi, cols=cols: e.dma_start(out=Qs[h, :, cols], in_=qst[qi][:]),
                                  reads=[qstr[qi]], writes=[R["Qs"]], dma=True)

                    wb, wbr = st.get(t * 8 + 0)
                    bq = [nbank(), nbank()]
                    for c in range(2):
                        group(wb, wbr, c * 128, (c + 1) * 128, bq[c][0], bq[c][1])
                    norm_stat([bq[0][0], bq[1][0]], [bq[0][1], bq[1][1]], 256, rq, rqr, P)
                    for c in range(2):
                        S.add("dve", lambda e, c=c, bq=bq: e.scalar_tensor_tensor(
                            out=qn[:, c, :], in0=bq[c][0], scalar=G[:, G_Q + l * 2 + c:G_Q + l * 2 + c + 1], in1=rq[:],
                            op0=ALU.mult, op1=ALU.mult), reads=[bq[c][1], Cr, rqr], writes=[qnr])
                    wb, wbr = st.get(t * 8 + 1)
                    bkv, bkvr = nbank()
                    group(wb, wbr, 0, 128, bkv, bkvr)
                    bkr_, bkrr = nbank()
                    group(wb, wbr, 128, 192, bkr_, bkrr, rows=64)
                    norm_stat([bkv], [bkvr], 128, rq, rqr, P)
                    S.add("dve", lambda e, bkv=bkv: e.scalar_tensor_tensor(
                        out=kvst[:], in0=bkv, scalar=G[:, G_KV + l:G_KV + l + 1], in1=rq[:],
                        op0=ALU.mult, op1=ALU.mult), reads=[bkvr, Cr, rqr], writes=[kvstr])
                    S.add(STQ, lambda e, cols=cols: e.dma_start(out=ag1s[:, cols], in_=kvst[:]), reads=[kvstr], writes=[R["ag1s"]], dma=True)
                    t1, t1r = ntmp()
                    t2, t2r = ntmp()
                    S.add("dve", lambda e, t1=t1, bkr_=bkr_: e.tensor_tensor(out=t1[0:32, :], in0=bkr_[0:32, :], in1=rM[0:32, :], op=ALU.mult),
                          reads=[bkrr, rMr], writes=[t1r])
                    S.add("dve", lambda e, t2=t2, bkr_=bkr_: e.tensor_tensor(out=t2[0:32, :], in0=bkr_[32:64, :], in1=rM[32:64, :], op=ALU.mult),
                          reads=[bkrr, rMr], writes=[t2r])
                    S.add("dve", lambda e, t1=t1, t2=t2: e.tensor_tensor(out=krst[:], in0=t1[0:32, :], in1=t2[0:32, :], op=ALU.add),
                          reads=[t1r, t2r], writes=[krstr])
                    S.add(STQ, lambda e, cols=cols: e.dma_start(out=ag2s[0:32, cols], in_=krst[:]), reads=[krstr], writes=[R["ag2s"]], dma=True)
                    q_heads(range(0, 4))
                    for hf in range(2):
                        wb, wbr = st.get(t * 8 + 2 + hf)
                        for c in range(2):
                            bk, bkr = nbank()
                            group(wb, wbr, c * 128, (c + 1) * 128, bk, bkr)
                            ci = hf * 2 + c
                            S.add("act", lambda e, ci=ci, bk=bk: e.activation(out=csb[:, ci, :], in_=bk, func=AF.Copy),
                                  reads=[bkr], writes=[csbr[ci]])
                    if t + 1 < NT:
                        rmsnorm_stat(t + 1, P, *rsl[(t + 1) % 2])
                    for hf in range(2):
                        wb, wbr = st.get(t * 8 + 4 + hf)
                        for c in range(2):
                            bk, bkr = nbank()
                            group(wb, wbr, c * 128, (c + 1) * 128, bk, bkr)
                            ci = hf * 2 + c
                            zi = zc[0] % 3
                            zc[0] += 1
                            S.add("dve", lambda e, ci=ci, bk=bk, zi=zi: e.tensor_tensor(out=zst[zi][:], in0=bk, in1=csb[:, ci, :], op=ALU.mult),
                                  reads=[bkr, csbr[ci]], writes=[zstr[zi]])
                            S.add(STQ, lambda e, ci=ci, zi=zi, t=t: e.dma_start(out=zs[ci, :, 1 + t * TT:1 + (t + 1) * TT], in_=zst[zi][:]),
                                  reads=[zstr[zi]], writes=[R["zs"]], dma=True)
                    wb, wbr = st.get(t * 8 + 6)
                    bA, bAr = nbank()
                    group(wb, wbr, 0, 128, bA, bAr)
                    bB, bBr = nbank()
                    group(wb, wbr, 128, 256, bB, bBr)
                    t1, t1r = ntmp()
                    t2, t2r = ntmp()
                    ki = t % 2
                    S.add("dve", lambda e, t1=t1, bA=bA: e.tensor_tensor(out=t1[:], in0=bA, in1=rC[:], op=ALU.mult), reads=[bAr, rCr], writes=[t1r])
                    S.add("dve", lambda e, t2=t2, bB=bB: e.tensor_tensor(out=t2[:], in0=bB, in1=rS[:], op=ALU.mult), reads=[bBr, rSr], writes=[t2r])
                    S.add("dve", lambda e, t1=t1, t2=t2, ki=ki: e.tensor_tensor(out=kst[ki][:], in0=t1[:], in1=t2[:], op=ALU.add),
                          reads=[t1r, t2r], writes=[kstr[ki]])
                    S.add(STQ, lambda e, ki=ki, t=t: e.dma_start(out=Ksc[:, 128 + t * TT:128 + (t + 1) * TT], in_=kst[ki][:]),
                          reads=[kstr[ki]], writes=[R["Ksc"]], dma=True)
                    wb, wbr = st.get(t * 8 + 7)
                    bk, bkr = nbank()
                    for blk in range(4):
                        for k in range(8):
                            S.add("pe", lambda e, k=k, blk=blk, bk=bk, wb=wb: e.matmul(
                                bk[:, blk * 128:(blk + 1) * 128], lhsT=P["xn"][:, k, blk * 128:(blk + 1) * 128], rhs=wb[:, k, 0:128],
                                start=(k == 0), stop=(k == 7)), reads=[wbr, P["xnr"][k]], writes=[bkr])
                    vi = t % 2
                    S.add("act", lambda e, bk=bk, vi=vi: e.activation(
                        out=vst[vi][:, :, :, 0:64], in_=bk.rearrange("p (b g c) -> p b g c", b=4, g=2), func=AF.Copy),
                        reads=[bkr], writes=[vstr[vi]])
                    S.add(STQ, lambda e, vi=vi, t=t: e.dma_start(
                        out=Vsc[128 + t * TT:128 + (t + 1) * TT, :].rearrange("(b p) c -> p b c", p=128),
                        in_=vst[vi][:].rearrange("p b g c -> p b (g c)")), reads=[vstr[vi]], writes=[R["Vsc"]], dma=True)
                    if t + 1 < NT:
                        rmsnorm_apply(t + 1, gcol, P, *rsl[(t + 1) % 2])
                    else:
                        pre_norm(nxt, P)
                    q_heads(range(4, 8))
                S.barrier()

        def exchange_start(l):
            K4 = lambda r0: ag2s[r0:r0 + 4, :].rearrange("a (b t) -> (a b) t", t=128)
            V8 = lambda r0: ag2s[r0:r0 + 8, :].rearrange("a (b c) -> (a b) c", c=256)
            S.add(STQ, lambda e: e.dma_start(out=K4(32), in_=Ksc[:, 128:256]), reads=[R["Ksc"]], writes=[R["ag2s"]], dma=True)
            S.add(STQ, lambda e: e.dma_start(out=K4(36), in_=Ksc[:, T:T + 128]), reads=[R["Ksc"]], writes=[R["ag2s"]], dma=True)
            S.add(STQ, lambda e: e.dma_start(out=V8(40), in_=Vsc[128:256, :]), reads=[R["Vsc"]], writes=[R["ag2s"]], dma=True)
            S.add(STQ, lambda e: e.dma_start(out=V8(48), in_=Vsc[T:T + 128, :]), reads=[R["Vsc"]], writes=[R["ag2s"]], dma=True)
            for ci in range(4):
                S.add(STQ, lambda e, ci=ci: e.dma_start(out=ag3s[:, 2 * ci:2 * ci + 1], in_=zs[ci, :, 1:2], allow_slow_non_contiguous=True),
                      reads=[R["zs"]], writes=[R["ag3s"]], dma=True)
                S.add(STQ, lambda e, ci=ci: e.dma_start(out=ag3s[:, 2 * ci + 1:2 * ci + 2], in_=zs[ci, :, T:T + 1], allow_slow_non_contiguous=True),
                      reads=[R["zs"]], writes=[R["ag3s"]], dma=True)
            GR = [[0, 1], [2, 3], [4, 5], [6, 7]]
            for (a, b_, rs_, ro) in ((ag1s, ag1o, "ag1s", "ag1o"), (ag2s, ag2o, "ag2s", "ag2o"), (ag3s, ag3o, "ag3s", "ag3o")):
                S.add("pool", lambda e, a=a, b_=b_: e.collective_compute(
                    "AllGather", ALU.bypass, replica_groups=GR, ins=[a.ap().opt()], outs=[b_.ap().opt()]),
                    reads=[R[rs_]], writes=[R[ro]], cc=True)

        def exchange_finish(l, ps):
            K4o = lambda r0: ag2o[r0:r0 + 4, :].rearrange("a (b t) -> (a b) t", t=128)
            V8o = lambda r0: ag2o[r0:r0 + 8, :].rearrange("a (b c) -> (a b) c", c=256)
            S.add("sp", lambda e: e.dma_start(out=Ksc[:, 0:128], in_=K4o(36)), reads=[R["ag2o"]], writes=[R["Ksc_h"]], dma=True)
            S.add("sp", lambda e: e.dma_start(out=Ksc[:, T + 128:T + 256], in_=K4o(64 + 32)), reads=[R["ag2o"]], writes=[R["Ksc_h"]], dma=True)
            vh = sb("vh", [128, 2, 256], BF16, ps); vhr = Res("vh")
            zh = sb("zh", [128, 2, 8], F32, ps); zhr = Res("zh")
            S.add("sp", lambda e: e.dma_start(out=vh[:, 0, :], in_=V8o(48)), reads=[R["ag2o"]], writes=[vhr], dma=True)
            S.add("sp", lambda e: e.dma_start(out=vh[:, 1, :], in_=V8o(64 + 40)), reads=[R["ag2o"]], writes=[vhr], dma=True)
            S.add("sp", lambda e: e.dma_start(out=zh[:, 0, :], in_=ag3o[0:128, :]), reads=[R["ag3o"]], writes=[zhr], dma=True)
            S.add("sp", lambda e: e.dma_start(out=zh[:, 1, :], in_=ag3o[128:256, :]), reads=[R["ag3o"]], writes=[zhr], dma=True)
            for i in range(2):
                S.add("dve", lambda e, i=i: e.tensor_scalar(out=vh[:, i, :], in0=vh[:, i, :], scalar1=CV[:, 64 + i:65 + i], scalar2=None, op0=ALU.mult),
                      reads=[vhr, Cr], writes=[vhr])
                S.add("dve", lambda e, i=i: e.tensor_scalar(out=zh[:, i, :], in0=zh[:, i, :], scalar1=CV[:, 64 + i:65 + i], scalar2=None, op0=ALU.mult),
                      reads=[zhr, Cr], writes=[zhr])
            S.add("sp", lambda e: e.dma_start(out=Vsc[0:128, :], in_=vh[:, 0, :]), reads=[vhr], writes=[R["Vsc_h"]], dma=True)
            S.add("sp", lambda e: e.dma_start(out=Vsc[T + 128:T + 256, :], in_=vh[:, 1, :]), reads=[vhr], writes=[R["Vsc_h"]], dma=True)
            for ci in range(4):
                S.add("sp", lambda e, ci=ci: e.dma_start(out=zs[ci, :, 0:1], in_=zh[:, 0, 2 * ci + 1:2 * ci + 2], allow_slow_non_contiguous=True), reads=[zhr], writes=[R["zs_h"]], dma=True)
                S.add("sp", lambda e, ci=ci: e.dma_start(out=zs[ci, :, T + 1:T + 2], in_=zh[:, 1, 2 * ci:2 * ci + 1], allow_slow_non_contiguous=True), reads=[zhr], writes=[R["zs_h"]], dma=True)

        def mla_phase(l):
            with contextlib.ExitStack() as ps:
                KVN = sb("KVN", [128, 2 * T], BF16, ps); KVNr = Res("KVN")
                KhT = sb("KhT", [96, 2 * T], BF16, ps); KhTr = Res("KhT"); KhTe = [Res("KhTe0"), Res("KhTe1")]
                VA = sb("VA", [128, 64, 128], BF16, ps); VAr = Res("VA")
                QhT = sb("QhT", [96, T], BF16, ps); QhTr = Res("QhT")
                wukv = sb("wukv", [128, 1024], BF16, ps); wukvr = Res("wukv")
                NPB = 3
                Pb = [sb("Pb%d" % i, [128, 2, TT], BF16, ps) for i in range(NPB)]
                Pbr = [Res("Pb%d" % i) for i in range(NPB)]
                rsb = [sb("rsb%d" % i, [64, TT], F32, ps) for i in range(1)] * 2
                rsbr = [Res("rsb%d" % i) for i in range(1)] * 2
                ost = [sb("ost%d" % i, [64, TT], BF16, ps) for i in range(1)] * 2
                ostr = [Res("ost%d" % i) for i in range(1)] * 2
                S.add("sp", lambda e: e.dma_start(out=wukv[:], in_=wukv_b[l]), reads=[ukvres[l]], writes=[wukvr], dma=True)
                for r in range(2):
                    S.add("sp", lambda e, r=r: e.dma_start(out=KVN[:, r * T:(r + 1) * T], in_=ag1o[r * 128:(r + 1) * 128, :]),
                          reads=[R["ag1o"]], writes=[KVNr], dma=True)
                    S.add("sp", lambda e, r=r: e.dma_start(out=KhT[64:96, r * T:(r + 1) * T], in_=ag2o[r * 64:r * 64 + 32, :]),
                          reads=[R["ag2o"]], writes=[KhTr], dma=True)
                for kc in range(64):
                    S.add("dve", lambda e, kc=kc: e.tensor_scalar(out=VA[:, kc, 64:128], in0=ones[:, 0:64], scalar1=CV[:, kc:kc + 1], scalar2=None, op0=ALU.mult),
                          reads=[Cr], writes=[VAr])
                SC = float(96 ** -0.5)
                pcnt = [0]
                ocnt = [0]
                sbank = [0]
                for h in range(8):
                    S.add("sp", lambda e, h=h: e.dma_start(out=QhT[:], in_=Qs[h]), reads=[R["Qs"]], writes=[QhTr], dma=True)
                    for kt in range(16):
                        bk, bkr = nbank()
                        S.add("pe", lambda e, kt=kt, h=h, bk=bk: e.matmul(bk[0:64, :], lhsT=wukv[:, h * 128:h * 128 + 64], rhs=KVN[:, kt * TT:(kt + 1) * TT],
                                                                   start=True, stop=True), reads=[wukvr, KVNr], writes=[bkr])
                        if kt % 2 == 0:
                            S.add("act", lambda e, kt=kt, bk=bk: e.activation(out=KhT[0:64, kt * TT:(kt + 1) * TT], in_=bk[0:64, :], func=AF.Copy),
                                  reads=[bkr], writes=[KhTe[0]])
                        else:
                            S.add("dve", lambda e, kt=kt, bk=bk: e.tensor_copy(out=KhT[0:64, kt * TT:(kt + 1) * TT], in_=bk[0:64, :]),
                                  reads=[bkr], writes=[KhTe[1]])
                    for kt in range(16):
                        bk, bkr = nbank()
                        for j in range(4):
                            kc = kt * 4 + j
                            S.add("pe", lambda e, kc=kc, j=j, h=h, bk=bk: e.matmul(bk[:, j * 64:(j + 1) * 64], lhsT=KVN[:, kc * 128:(kc + 1) * 128],
                                                                             rhs=wukv[:, h * 128 + 64:h * 128 + 128], start=True, stop=True),
                                  reads=[wukvr, KVNr], writes=[bkr])
                        S.add("dve", lambda e, kt=kt, bk=bk: e.tensor_tensor(
                            out=VA[:, kt * 4:kt * 4 + 4, 0:64], in0=bk[:, 0:256].rearrange("p (j c) -> p j c", j=4),
                            in1=CV[:, kt * 4:kt * 4 + 4].unsqueeze(2).broadcast_to([128, 4, 64]), op=ALU.mult),
                            reads=[bkr, Cr], writes=[VAr])
                    for qt in range(NT):
                        qcols = slice(qt * TT, (qt + 1) * TT)
                        oi = ocnt[0] % 2
                        ocnt[0] += 1
                        ob, obr = PS[:, oi, :], bankr[oi]
                        SK = 2
                        pend = []
                        for kp in range(32 + SK):
                            if kp < 32:
                                pr = sbank[0] % 3
                                sbank[0] += 1
                                b0 = 2 + 2 * pr
                                for j in range(2):
                                    kc = 2 * kp + j
                                    S.add("pe", lambda e, kc=kc, bj=b0 + j, qcols=qcols: e.matmul(PS[:, bj, :], lhsT=KhT[:, kc * 128:(kc + 1) * 128], rhs=QhT[:, qcols],
                                                                                           start=True, stop=True), reads=[KhTr, KhTe[0], KhTe[1], QhTr], writes=[bankr[b0 + j]])
                                pi = pcnt[0] % NPB
                                pcnt[0] += 1
                                S.add("act", lambda e, b0=b0, pi=pi: e.activation(out=Pb[pi][:], in_=PS[:, b0:b0 + 2, :], func=AF.Exp, scale=SC),
                                      reads=[bankr[b0], bankr[b0 + 1]], writes=[Pbr[pi]])
                                pend.append(pi)
                            if kp >= SK:
                                k2 = kp - SK
                                pi = pend[k2]
                                for j in range(2):
                                    kc = 2 * k2 + j
                                    S.add("pe", lambda e, kc=kc, j=j, pi=pi, ob=ob: e.matmul(ob, lhsT=VA[:, kc, :], rhs=Pb[pi][:, j, :], start=(kc == 0), stop=(kc == 63)),
                                          reads=[VAr, Pbr[pi]], writes=[obr])
                        S.add("dve", lambda e, ob=ob, oi=oi: e.reciprocal(out=rsb[oi][:], in_=ob[64:128, :]), reads=[obr], writes=[rsbr[oi]])
                        S.add("dve", lambda e, ob=ob, oi=oi: e.tensor_tensor(out=ost[oi][:], in0=ob[0:64, :], in1=rsb[oi][:], op=ALU.mult),
                              reads=[obr, rsbr[oi]], writes=[ostr[oi]])
                        S.add("sp", lambda e, oi=oi, h=h, qcols=qcols: e.dma_start(out=Os[h * 64:(h + 1) * 64, qcols], in_=ost[oi][:]),
                              reads=[ostr[oi]], writes=[R["Os"]], dma=True)
                S.barrier()

        def pass2a(l, nxt=None):
            with contextlib.ExitStack() as ps:
                P = {}
                norm_bufs(ps, P)
                gcol = G_MIX + l * 8
                st = slab_stream(ps, "wb", "inB", l, 6, [128, 8, 256], 3, 2)
                ZW = [sb("ZW%d" % i, [128, TT + 2], F32, ps) for i in range(2)]
                ZWr = [Res("ZW%d" % i) for i in range(2)]
                ycb = [sb("ycb%d" % i, [128, TT], F32, ps) for i in range(2)]
                ycbr = [Res("ycb%d" % i) for i in range(2)]
                byst = sb("byst", [128, 4, TT], BF16, ps); bystr = Res("byst")
                Qsw = sb("Qsw", [128, 4, TT], BF16, ps); Qswr = Res("Qsw")
                Osw = sb("Osw", [128, 4, TT], BF16, ps); Oswr = Res("Osw")
                KW = sb("KW", [128, 2, 768], BF16, ps); KWr = Res("KW")
                VW = sb("VW", [128, 6, 256], BF16, ps); VWr = Res("VW")
                NPB = 12
                Pb = [sb("Pw%d" % i, [128, 256], BF16, ps) for i in range(NPB)]
                Pbr = [Res("Pw%d" % i) for i in range(NPB)]
                rC = sb("rC", [128, TT], F32, ps); rCr = Res("rC")
                rS = sb("rS", [128, TT], F32, ps); rSr = Res("rS")
                tmp = [sb("tmp%d" % i, [128, TT], F32, ps) for i in range(4)]
                tmpr = [Res("tmp%d" % i) for i in range(4)]
                rsb = [sb("rsw%d" % i, [128, 256], F32, ps) for i in range(4)]
                rsbr = [Res("rsw%d" % i) for i in range(4)]
                Esk = sb("Esk", [8, 128], BF16, ps); Eskr = Res("Esk")
                rs2 = sb("rstd2a", [128, TT], F32, ps)
                rsl = [(P["rstd"], P["rstdr"]), (rs2, Res("rstd2a"))]
                S.add("dve", lambda e: e.tensor_scalar(out=Esk[:], in0=sel01[:], scalar1=sinkT[:, l:l + 1], scalar2=None, op0=ALU.mult),
                      reads=[Cr], writes=[Eskr])
                cwc = G_CW + l * 12
                pcnt = [0]
                rcnt = [0]
                zcnt = [0]
                order = [1, 2, 3, 4, 5, 6, 0, 7]
                for ti, t in enumerate(order):
                    cols = slice(t * TT, (t + 1) * TT)
                    if ti == 6:
                        exchange_finish(l, ps)
                    edge = t in (0, NT - 1)
                    st.get(ti * 6)
                    S.add("sp", lambda e, cols=cols: e.dma_start(out=rC[:], in_=ropeSC_d[:, cols]), writes=[rCr], dma=True)
                    S.add("sp", lambda e, cols=cols: e.dma_start(out=rS[:], in_=ropeSS_d[:, cols]), writes=[rSr], dma=True)
                    for dup in range(2):
                        for g in range(2):
                            S.add("sp", lambda e, dup=dup, g=g, t=t: e.dma_start(out=KW[dup * 64:(dup + 1) * 64, g, :], in_=Ksc[g * 64:(g + 1) * 64, t * TT:t * TT + 768]),
                                  reads=[R["Ksc"]] + ([R["Ksc_h"]] if edge else []), writes=[KWr], dma=True)
                    S.add("sp", lambda e, t=t: e.dma_start(out=VW[:], in_=Vsc[t * TT:t * TT + 768, :].rearrange("(b p) c -> p b c", p=128)),
                          reads=[R["Vsc"]] + ([R["Vsc_h"]] if edge else []), writes=[VWr], dma=True)
                    if ti == 0:
                        first_norm(t, gcol, P)

                    def group(wb, wbr, c0, c1, bk, bkr):
                        for k in range(8):
                            S.add("pe", lambda e, k=k: e.matmul(bk, lhsT=wb[:, k, c0:c1], rhs=P["xn"][:, k, :], start=(k == 0), stop=(k == 7)),
                                  reads=[wbr, P["xnr"][k]], writes=[bkr])

                    for hf in range(2):
                        wb, wbr = st.get(ti * 6 + hf)
                        for c in range(2):
                            ci = hf * 2 + c
                            bk, bkr = nbank()
                            group(wb, wbr, c * 128, (c + 1) * 128, bk, bkr)
                            zi = zcnt[0] % 2
                            zcnt[0] += 1
                            S.add("sp", lambda e, ci=ci, zi=zi, t=t: e.dma_start(out=ZW[zi][:], in_=zs[ci, :, t * TT:t * TT + TT + 2]),
                                  reads=[R["zs"]] + ([R["zs_h"]] if edge else []), writes=[ZWr[zi]], dma=True)
                            yc, ycr = ycb[zi], ycbr[zi]
                            w0 = cwc + ci * 3
                            S.add("dve", lambda e, zi=zi, yc=yc, w0=w0: e.tensor_scalar(out=yc[:], in0=ZW[zi][:, 0:TT], scalar1=G[:, w0:w0 + 1], scalar2=None, op0=ALU.mult),
                                  reads=[ZWr[zi], Cr], writes=[ycr])
                            S.add("dve", lambda e, zi=zi, yc=yc, w0=w0: e.scalar_tensor_tensor(out=yc[:], in0=ZW[zi][:, 1:TT + 1], scalar=G[:, w0 + 1:w0 + 2], in1=yc[:],
                                                                                          op0=ALU.mult, op1=ALU.add), reads=[ZWr[zi], Cr, ycr], writes=[ycr])
                            S.add("dve", lambda e, zi=zi, yc=yc, w0=w0: e.scalar_tensor_tensor(out=yc[:], in0=ZW[zi][:, 2:TT + 2], scalar=G[:, w0 + 2:w0 + 3], in1=yc[:],
                                                                                          op0=ALU.mult, op1=ALU.add), reads=[ZWr[zi], Cr, ycr], writes=[ycr])
                            S.add("dve", lambda e, ci=ci, bk=bk, yc=yc: e.tensor_tensor(out=byst[:, ci, :], in0=bk, in1=yc[:], op=ALU.mult),
                                  reads=[bkr, ycr], writes=[bystr])
                    S.add(STQ, lambda e, cols=cols: e.dma_start(out=BYs.rearrange("(c p) t -> p c t", p=128)[:, :, cols], in_=byst[:]),
                          reads=[bystr], writes=[R["BYs"]], dma=True)
                    if ti + 1 < NT:
                        rmsnorm_stat(order[ti + 1], P, *rsl[(ti + 1) % 2])
                    for i in range(4):
                        wb, wbr = st.get(ti * 6 + 2 + i)
                        bA, bAr = nbank()
                        group(wb, wbr, 0, 128, bA, bAr)
                        bB, bBr = nbank()
                        group(wb, wbr, 128, 256, bB, bBr)
                        ta, tb = 2 * (i % 2), 2 * (i % 2) + 1
                        S.add("dve", lambda e, bA=bA, ta=ta: e.tensor_tensor(out=tmp[ta][:], in0=bA, in1=rC[:], op=ALU.mult), reads=[bAr, rCr], writes=[tmpr[ta]])
                        S.add("dve", lambda e, bB=bB, tb=tb: e.tensor_tensor(out=tmp[tb][:], in0=bB, in1=rS[:], op=ALU.mult), reads=[bBr, rSr], writes=[tmpr[tb]])
                        S.add("dve", lambda e, i=i, ta=ta, tb=tb: e.tensor_tensor(out=Qsw[:, i, :], in0=tmp[ta][:], in1=tmp[tb][:], op=ALU.add),
                              reads=[tmpr[ta], tmpr[tb]], writes=[Qswr])
                    if ti + 1 < NT:
                        rmsnorm_apply(order[ti + 1], gcol, P, *rsl[(ti + 1) % 2])
                    else:
                        pre_norm(nxt, P)
                    def stage1(b, g, par):
                        p0, p1 = par * 64, (par + 1) * 64
                        pis = []
                        for kb in range(3):
                            sbk, sbkr = nbank()
                            S.add("pe", lambda e, sbk=sbk, kb=kb: e.matmul(
                                sbk[:, 0:256], lhsT=KW[p0:p1, g, (b + kb) * 128:(b + kb + 1) * 128],
                                rhs=Qsw[p0:p1, 2 * g:2 * g + 2, b * 128:(b + 1) * 128], start=True, stop=True),
                                reads=[KWr, Qswr], writes=[sbkr])
                            pi = pcnt[0] % NPB
                            pcnt[0] += 1
                            pis.append(pi)
                            S.add("act", lambda e, sbk=sbk, pi=pi: e.activation(out=Pb[pi][:], in_=sbk[:, 0:256], func=AF.Exp, scale=0.125),
                                  reads=[sbkr], writes=[Pbr[pi]])
                            if kb != 1:
                                mi = 0 if kb == 0 else 1
                                S.add("dve", lambda e, pi=pi, mi=mi: e.tensor_tensor(
                                    out=Pb[pi][:].rearrange("p (j q) -> p j q", j=2), in0=Pb[pi][:].rearrange("p (j q) -> p j q", j=2),
                                    in1=masks[:, mi, :].unsqueeze(1).broadcast_to([128, 2, 128]), op=ALU.mult),
                                    reads=[Pbr[pi], Cr], writes=[Pbr[pi]])
                        return pis

                    def stage2(b, g, par, pis):
                        p0, p1 = par * 64, (par + 1) * 64
                        ob, obr = nbank()
                        for kb in range(3):
                            S.add("pe", lambda e, kb=kb, pi=pis[kb]: e.matmul(
                                ob[:, 0:256], lhsT=VW[:, b + kb, g * 128:(g + 1) * 128], rhs=Pb[pi][:], start=(kb == 0), stop=False),
                                reads=[VWr, Pbr[pis[kb]]], writes=[obr])
                        S.add("pe", lambda e: e.matmul(ob[:, 0:256], lhsT=Esk[:], rhs=rsel[:, g * 2 + par, :], start=False, stop=True),
                              reads=[Eskr, Cr], writes=[obr])
                        ri = rcnt[0] % 4
                        rcnt[0] += 1
                        S.add("act", lambda e: e.activation(out=rsb[ri][64:128, :], in_=ob[64:128, 0:256], func=AF.Ln), reads=[obr], writes=[rsbr[ri]])
                        S.add("act", lambda e: e.activation(out=rsb[ri][64:128, :], in_=rsb[ri][64:128, :], func=AF.Exp, scale=-1.0), reads=[rsbr[ri]], writes=[rsbr[ri]])
                        S.add("dve", lambda e: e.tensor_tensor(
                            out=Osw[p0:p1, 2 * g:2 * g + 2, b * 128:(b + 1) * 128], in0=ob[0:64, 0:256].rearrange("p (j q) -> p j q", j=2),
                            in1=rsb[ri][64:128, :].rearrange("p (j q) -> p j q", j=2), op=ALU.mult),
                            reads=[obr, rsbr[ri]], writes=[Oswr])

                    prev = None
                    for b in range(4):
                        for g in range(2):
                            for par in range(2):
                                pis = stage1(b, g, par)
                                if prev is not None:
                                    stage2(*prev)
                                prev = (b, g, par, pis)
                    stage2(*prev)
                    S.add(STQ, lambda e, cols=cols: e.dma_start(out=OSWs.rearrange("(c p) t -> p c t", p=128)[:, :, cols], in_=Osw[:]),
                          reads=[Oswr], writes=[R["OSWs"]], dma=True)
                S.barrier()

        def pass2b(l, nxt=None):
            with contextlib.ExitStack() as ps:
                P = {}
                norm_bufs(ps, P)
                gcol = G_MIX + l * 8
                gs = slab_stream(ps, "wg", "inG", l, 8, [128, 8, 384], 2, 1)
                ys = slab_stream(ps, "wy", "wy", l, 8, [128, 12, 128], 2, 1)
                os_ = slab_stream(ps, "wo", "wo", l, 8, [128, 8, 128], 3, 2)
                yin = sb("yin", [128, 12, TT], BF16, ps); yinr = Res("yin")
                sg = [sb("sg%d" % i, [128, 3, TT], F32, ps) for i in range(2)]
                sgr = [Res("sg%d" % i) for i in range(2)]
                mt = [sb("mt%d" % i, [128, TT], F32, ps) for i in range(3)]
                mtr = [Res("mt%d" % i) for i in range(3)]
                merged = sb("merged", [128, 8, TT], BF16, ps)
                mergedr = [Res("merged%d" % i) for i in range(8)]
                for t in range(NT):
                    cols = slice(t * TT, (t + 1) * TT)
                    gs.get(t * 8)
                    ys.get(t * 8)
                    for i, (src, rn) in enumerate(((Os, "Os"), (BYs, "BYs"), (OSWs, "OSWs"))):
                        S.add("sp", lambda e, i=i, src=src, cols=cols: e.dma_start(out=yin[:, 4 * i:4 * i + 4, :], in_=src.rearrange("(c p) t -> p c t", p=128)[:, :, cols]),
                              reads=[R[rn]], writes=[yinr], dma=True)
                    if t == 0:
                        first_norm(0, gcol, P)
                    for m in range(8):
                        gb, gbr = gs.get(t * 8 + m)
                        yb_, ybr_ = ys.get(t * 8 + m)
                        if m == 6:
                            os_.get(t * 8)
                        si = m % 2
                        gbk = []
                        for q in range(3):
                            bk, bkr = nbank()
                            for k in range(8):
                                S.add("pe", lambda e, k=k, q=q, bk=bk, gb=gb: e.matmul(bk, lhsT=gb[:, k, q * 128:(q + 1) * 128], rhs=P["xn"][:, k, :], start=(k == 0), stop=(k == 7)),
                                      reads=[gbr, P["xnr"][k]], writes=[bkr])
                            S.add("act", lambda e, q=q, bk=bk, si=si: e.activation(out=sg[si][:, q, :], in_=bk, func=AF.Sigmoid), reads=[bkr], writes=[sgr[si]])
                        for q in range(3):
                            bk, bkr = nbank()
                            for k in range(4):
                                S.add("pe", lambda e, k=k, q=q, bk=bk, yb_=yb_: e.matmul(bk, lhsT=yb_[:, 4 * q + k, :], rhs=yin[:, 4 * q + k, :], start=(k == 0), stop=(k == 3)),
                                      reads=[ybr_, yinr], writes=[bkr])
                            S.add("dve", lambda e, q=q, bk=bk, si=si: e.tensor_tensor(out=mt[q][:], in0=bk, in1=sg[si][:, q, :], op=ALU.mult),
                                  reads=[bkr, sgr[si]], writes=[mtr[q]])
                        S.add("dve", lambda e: e.tensor_tensor(out=mt[0][:], in0=mt[0][:], in1=mt[1][:], op=ALU.add), reads=[mtr[0], mtr[1]], writes=[mtr[0]])
                        S.add("dve", lambda e, m=m: e.tensor_tensor(out=merged[:, m, :], in0=mt[0][:], in1=mt[2][:], op=ALU.add), reads=[mtr[0], mtr[2]], writes=[mergedr[m]])
                    if t + 1 < NT:
                        rmsnorm_tile(t + 1, gcol, P)
                    else:
                        pre_norm(nxt, P)
                    for m in range(8):
                        ob_, obr_ = os_.get(t * 8 + m)
                        bk, bkr = nbank()
                        for k in range(8):
                            S.add("pe", lambda e, k=k, bk=bk, ob_=ob_: e.matmul(bk, lhsT=ob_[:, k, :], rhs=merged[:, k, :], start=(k == 0), stop=(k == 7)),
                                  reads=[obr_, mergedr[k]], writes=[bkr])
                        S.add("dve", lambda e, m=m, bk=bk, cols=cols: e.tensor_tensor(out=X[:, m, cols], in0=bk, in1=X[:, m, cols], op=ALU.add),
                              reads=[bkr, Xr[m][t]], writes=[Xr[m][t]])
                S.barrier()

        def final_phase():
            gcol = G_FIN
            with contextlib.ExitStack() as ps:
                P = {}
                P["sq"] = [sb("fsq%d" % i, [128, TT], BF16, ps) for i in range(2)]
                P["sqr"] = [Res("fsq%d" % i) for i in range(2)]
                rsb = [sb("frstd%d" % i, [128, TT], F32, ps) for i in range(2)]
                rsbr = [Res("frstd%d" % i) for i in range(2)]
                yb = [sb("yb%d" % i, [128, NC8, TT], F32, ps) for i in range(2)]
                ybr = [Res("yb%d" % i) for i in range(2)]
                for t in range(NT):
                    cols = slice(t * TT, (t + 1) * TT)
                    rs, rsr = rsb[t % 2], rsbr[t % 2]
                    norm_stat([X[:, c, cols] for c in range(NC8)], [Xr[c][t] for c in range(NC8)], D, rs, rsr, P)
                    y, yr = yb[t % 2], ybr[t % 2]
                    for c in range(NC8):
                        S.add("dve", lambda e, c=c, y=y, rs=rs, cols=cols: e.scalar_tensor_tensor(
                            out=y[:, c, :], in0=X[:, c, cols], scalar=G[:, gcol + c:gcol + c + 1], in1=rs[:],
                            op0=ALU.mult, op1=ALU.mult),
                            reads=[Xr[c][t], Cr, rsr], writes=[yr])
                    S.add("sp", lambda e, y=y, cols=cols: e.dma_start(
                        out=yT.rearrange("(c p) t -> p c t", p=128)[:, :, cols], in_=y[:]),
                        reads=[yr], dma=True)
                S.barrier()

        order = ["gu1", "dn1", "inA", "inB", "inG", "wy", "wo", "gu2", "dn2"]
        need = set()
        if cfg.ffn1:
            need |= {"gu1", "dn1"}
        if cfg.ffn2:
            need |= {"gu2", "dn2"}
        if cfg.mixer:
            need |= {"inA", "inB", "inG", "wy", "wo"}
        def cast_layer(l):
            for nm in order:
                if nm in need:
                    cast_w(nm, l)
            ukvres[l] = Res("wukv_%d" % l)
            S.add("pool", lambda e: e.dma_start(out=wukv_b[l], in_=wukv_d[l]), writes=[ukvres[l]], dma=True)

        cast_layer(0)
        for l in range(cfg.layers):
            full = cfg.ffn1 and cfg.mixer and cfg.p2a and cfg.mla and cfg.p2b and cfg.ffn2
            gm = G_MIX + l * 8
            if cfg.ffn1:
                ffn_phase(l, 1, (0, gm) if full else None)
            if cfg.mixer:
                pass1(l, (1, gm) if full else None)
                exchange_start(l)
                if cfg.p2a:
                    pass2a(l, (0, gm) if full else None)
                if l + 1 < cfg.layers:
                    cast_layer(l + 1)
                if cfg.mla:
                    mla_phase(l)
                if cfg.p2b:
                    pass2b(l, (0, G_F2 + l * 8) if full else None)
            if cfg.ffn2:
                ffn_phase(l, 2, (0, G_F1 + (l + 1) * 8) if (full and l + 1 < cfg.layers) else None)
        if cfg.final:
            final_phase()
        else:
            for t in range(NT):
                S.add("sp", lambda e, t=t: e.dma_start(
                    out=yT.rearrange("(c p) t -> p c t", p=128)[:, :, t * TT:(t + 1) * TT],
                    in_=X[:, :, t * TT:(t + 1) * TT]),
                    reads=[Xr[c][t] for c in range(NC8)], dma=True)
        S.emit(nc, es)
    return nc


_NC_CACHE = {}


def kernel(**inputs):
    cfg = Cfg()
    if "nc" not in _NC_CACHE:
        _NC_CACHE["nc"] = build(cfg)
    nc = _NC_CACHE["nc"]
    in_maps = prep_inputs(inputs)
    res = run_bass_kernel_spmd(nc, in_maps, core_ids=list(range(8)))
    outs = [np.ascontiguousarray(np.asarray(r["yT"]).T) for r in res.results]
    y_prompt = np.stack(outs[:4], axis=0)
    y_sample = np.stack([np.concatenate([outs[4], outs[5]], axis=0),
                         np.concatenate([outs[6], outs[7]], axis=0)], axis=0)
    return (y_prompt.astype(np.float32), y_sample.astype(np.float32))
```

```python
import contextlib
import numpy as np
import ml_dtypes
import concourse.bass as bass
import concourse.mybir as mybir
from concourse.bass_utils import run_bass_kernel_spmd

F32 = mybir.dt.float32
BF16 = mybir.dt.bfloat16
AF = mybir.ActivationFunctionType
ALU = mybir.AluOpType

D = 1024
DFF = 2816
L = 4
T = 4096
TT = 512
NT = T // TT
NC8 = D // 128
NJ = DFF // 128
EPS = 1e-6


class Res:
    __slots__ = ("name", "w", "r")

    def __init__(self, name):
        self.name = name
        self.w = None
        self.r = {}


class Op:
    __slots__ = ("eng", "fn", "deps", "signaled", "dma", "sem", "val", "pre", "idx", "cc")

    def __init__(self, eng, fn, dma, cc=False):
        self.eng = eng
        self.fn = fn
        self.dma = dma or cc
        self.cc = cc
        self.deps = []
        self.signaled = dma or cc
        self.sem = None
        self.val = 0
        self.pre = None
        self.idx = 0


ENGS = ("pe", "act", "dve", "pool", "sp")
NDMA = {"sp": 12, "pool": 8, "act": 4}


class Sched:
    def __init__(self):
        self.ops = {e: [] for e in ENGS}
        self.last = {e: None for e in ENGS}
        self.dmas = {q: [] for q in NDMA}
        self.pending_barrier = {e: None for e in ENGS}
        self.ccs = []
        self.n = 0

    def add(self, eng, fn, reads=(), writes=(), dma=False, cc=False):
        op = Op(eng, fn, dma, cc)
        dma = dma or cc
        op.idx = self.n
        self.n += 1
        deps = {}

        def dep(o, war=False, waw=False):
            if o is None or o is op:
                return
            if (not o.dma) and (not dma) and o.eng == eng:
                if eng == "pe" or war or waw:
                    return
            deps[id(o)] = o

        pb = self.pending_barrier[eng]
        if pb is not None:
            for o in pb:
                dep(o)
            self.pending_barrier[eng] = None
        for r in reads:
            dep(r.w)
        rset = set(id(r) for r in reads)
        for r in writes:
            dep(r.w, waw=(id(r) not in rset))
            for o in r.r.values():
                dep(o, war=True)
        key = ("dma", op.idx) if dma else eng
        for r in reads:
            r.r[key] = op
        for r in writes:
            r.w = op
            r.r = {}
        op.deps = list(deps.values())
        for o in op.deps:
            o.signaled = True
        self.ops[eng].append(op)
        if cc:
            self.ccs.append(op)
        elif dma:
            self.dmas[eng].append(op)
        else:
            self.last[eng] = op
        return op

    def barrier(self):
        outstanding = []
        for e in ENGS:
            if self.last[e] is not None:
                outstanding.append(self.last[e])
        for q, lst in self.dmas.items():
            outstanding.extend(lst[-NDMA[q]:])
        outstanding.extend(self.ccs[-3:])
        for e in ENGS:
            self.pending_barrier[e] = list(outstanding)

    def emit(self, nc, es):
        sems = {e: es.enter_context(nc.semaphore("sem_" + e)) for e in ("pe", "act", "dve", "pool")}
        dsems = {q: [es.enter_context(nc.semaphore("dsem_%s%d" % (q, i))) for i in range(n)]
                 for q, n in NDMA.items()}
        for e in ENGS:
            cnt = 0
            di = 0
            for op in self.ops[e]:
                if op.cc:
                    op.sem = es.enter_context(nc.semaphore("ccsem%d" % op.idx))
                    op.val = 1
                elif op.dma:
                    n = NDMA[e]
                    op.sem = dsems[e][di % n]
                    op.val = 16 * (di // n + 1)
                    if di >= n:
                        op.pre = (op.sem, 16 * (di // n))
                    di += 1
                elif op.signaled:
                    cnt += 1
                    op.sem = sems[e]
                    op.val = cnt
        block = es.enter_context(nc.Block())
        handles = {"pe": block.tensor, "act": block.scalar, "dve": block.vector,
                   "pool": block.gpsimd, "sp": block.sync}

        def make(e):
            def body(eng):
                waited = {}

                def wait(sem, val):
                    k = id(sem)
                    if waited.get(k, 0) < val:
                        eng.wait_ge(sem, val)
                        waited[k] = val

                for op in self.ops[e]:
                    need = {}
                    for d in op.deps:
                        k = id(d.sem)
                        if k not in need or need[k][1] < d.val:
                            need[k] = (d.sem, d.val)
                    if op.pre is not None:
                        k = id(op.pre[0])
                        if k not in need or need[k][1] < op.pre[1]:
                            need[k] = op.pre
                    for sem, val in need.values():
                        wait(sem, val)
                    ins = op.fn(eng)
                    if op.cc:
                        ins.then_inc(op.sem)
                    elif op.dma:
                        ins.then_inc(op.sem, 16)
                    elif op.signaled:
                        ins.then_inc(op.sem, 1)
                if e in NDMA:
                    for op in self.dmas[e][-NDMA[e]:]:
                        wait(op.sem, op.val)
            return body

        for e in ENGS:
            handles[e](make(e))


QL0, KV0, KR0, CB0, CC0, CX0, SQ0, SK0, SV0, GA0, GB0, GC0 = 0, 256, 384, 416, 928, 1440, 1952, 2464, 2592, 2720, 3744, 4768
SWAP32 = np.concatenate([np.arange(16, 32), np.arange(0, 16)])
SWAP64 = np.concatenate([np.arange(32, 64), np.arange(0, 32)])
SWAP64X2 = np.concatenate([SWAP64, 64 + SWAP64])
AR = np.arange
SLABS_A = [
    QL0 + AR(256),
    np.concatenate([KV0 + AR(128), KR0 + AR(32), KR0 + SWAP32, KR0 + AR(32), KR0 + AR(32)]),
    CC0 + AR(256), CC0 + 256 + AR(256), CX0 + AR(256), CX0 + 256 + AR(256),
    np.concatenate([SK0 + AR(128), SK0 + SWAP64X2]),
    np.concatenate([SV0 + AR(128), SV0 + AR(128)]),
]
SLABS_B = [CB0 + AR(256), CB0 + 256 + AR(256)] + [
    np.concatenate([SQ0 + i * 128 + AR(128), SQ0 + i * 128 + SWAP64X2]) for i in range(4)]
SLABS_G = [np.concatenate([GA0 + m * 128 + AR(128), GB0 + m * 128 + AR(128), GC0 + m * 128 + AR(128)]) for m in range(8)]


def lay_gu(w):
    Lw = w.shape[0]
    a = w[:, :, :DFF].reshape(Lw, 8, 128, NJ, 128)
    b = w[:, :, DFF:].reshape(Lw, 8, 128, NJ, 128)
    ab = np.stack([a, b], axis=4)
    return np.ascontiguousarray(ab.transpose(0, 3, 2, 1, 4, 5)).reshape(Lw, NJ, 128, 8 * 256)


def lay_down(w):
    Lw = w.shape[0]
    x = w.reshape(Lw, NJ, 128, 8, 128)
    return np.ascontiguousarray(x.transpose(0, 3, 2, 1, 4)).reshape(Lw, 8, 128, NJ * 128)


def lay_colslabs(w, slabs):
    Lw = w.shape[0]
    out = []
    for cols in slabs:
        x = w[:, :, cols].reshape(Lw, 8, 128, len(cols)).transpose(0, 2, 1, 3)
        out.append(x.reshape(Lw, 128, 8 * len(cols)))
    return np.ascontiguousarray(np.stack(out, axis=1))


def lay_rows(w, nk):
    Lw, C = w.shape[0], w.shape[2]
    return w.reshape(Lw, nk, 128, C).transpose(0, 2, 1, 3)


def lay_vec(v):
    Lw, n = v.shape[0], v.shape[1] // 128
    return np.ascontiguousarray(v.reshape(Lw, n, 128).transpose(2, 0, 1)).reshape(128, Lw * n)


G_F1, G_MIX, G_F2, G_FIN, G_Q, G_KV, G_CW, NV = 0, 32, 64, 96, 104, 112, 116, 164


def rope_tab(pos, dim):
    inv = (1.0 / (np.float32(10000.0) ** (np.arange(0, dim, 2, dtype=np.float32) / np.float32(dim)))).astype(np.float32)
    ang = (pos.astype(np.float32)[:, None] * inv[None, :]).astype(np.float32)
    return np.cos(ang).astype(np.float32).T, np.sin(ang).astype(np.float32).T


def prep_inputs(inp):
    f = lambda k: np.asarray(inp[k], dtype=np.float32)
    vecs = np.zeros((128, NV), np.float32)
    vecs[:, G_F1:G_F1 + 32] = lay_vec(f("ffn1_norm"))
    vecs[:, G_MIX:G_MIX + 32] = lay_vec(f("mix_norm"))
    vecs[:, G_F2:G_F2 + 32] = lay_vec(f("ffn2_norm"))
    vecs[:, G_FIN:G_FIN + 8] = lay_vec(f("final_norm")[None, :])
    vecs[:, G_Q:G_Q + 8] = lay_vec(f("mla_q_norm"))
    vecs[:, G_KV:G_KV + 4] = lay_vec(f("mla_kv_norm"))
    cw = f("conv_w").reshape(L, 3, 4, 128).transpose(3, 0, 2, 1)
    vecs[:, G_CW:G_CW + 48] = cw.reshape(128, 48)
    w_in = f("w_in")
    wuq = f("mla_w_uq")
    uq = np.zeros((L, 128, 2, 8, 128), np.float32)
    for h in range(8):
        cols = np.concatenate([h * 96 + AR(64), h * 96 + 64 + AR(32), h * 96 + 64 + SWAP32])
        uq[:, :, :, h, :] = wuq[:, :, cols].reshape(L, 2, 128, 128).transpose(0, 2, 1, 3)
    wy = np.concatenate([lay_rows(f("mla_w_o"), 4), lay_rows(f("conv_w_o"), 4), lay_rows(f("swa_w_o"), 4)], axis=2)
    wy = wy.reshape(L, 128, 12, 8, 128).transpose(0, 3, 1, 2, 4).reshape(L, 8, 128, 12 * 128)
    wo = lay_rows(f("w_o"), 8).reshape(L, 128, 8, 8, 128).transpose(0, 3, 1, 2, 4).reshape(L, 8, 128, 8 * 128)
    masks = np.zeros((128, 2, 128), np.float32)
    cc, aa = np.meshgrid(np.arange(128), np.arange(128), indexing="ij")
    masks[:, 0, :] = (cc >= aa)
    masks[:, 1, :] = (cc <= aa)
    sel01 = np.zeros((8, 128), np.float32)
    sel01[:, 64:] = 1.0
    rsel = np.zeros((8, 4, 256), np.float32)
    for g in range(2):
        for par in range(2):
            for j in range(2):
                rsel[4 * g + par + 2 * j, g * 2 + par, j * 128:(j + 1) * 128] = 1.0
    shared = {
        "wgu1": lay_gu(f("ffn1_w_gu")), "wdn1": lay_down(f("ffn1_w_down")),
        "wgu2": lay_gu(f("ffn2_w_gu")), "wdn2": lay_down(f("ffn2_w_down")),
        "winA": lay_colslabs(w_in, SLABS_A), "winB": lay_colslabs(w_in, SLABS_B), "winG": lay_colslabs(w_in, SLABS_G),
        "wuq": np.ascontiguousarray(uq.reshape(L, 128, 2 * 8 * 128)),
        "wukv": np.ascontiguousarray(f("mla_w_ukv")),
        "wy": np.ascontiguousarray(wy), "wo": np.ascontiguousarray(wo),
        "vecs": vecs, "sinkT": np.ascontiguousarray(f("swa_sink").T),
        "masks": masks, "sel01": sel01, "rsel": rsel,
    }
    xp, xs = f("x_prompt"), f("x_sample")
    per_core = []
    for c in range(8):
        rank = c % 2
        cvec = np.zeros((128, 66), np.float32)
        if c < 4:
            xc = xp[c]
            off = 0
            cvec[:, rank * 32:(rank + 1) * 32] = 1.0
        else:
            i = c - 4
            xc = xs[i // 2, rank * T:(rank + 1) * T]
            off = rank * T
            cvec[:, 0:64] = 1.0
            cvec[:, 64] = 1.0 if rank == 1 else 0.0
            cvec[:, 65] = 1.0 if rank == 0 else 0.0
        pos = np.arange(off, off + T)
        c32, s32 = rope_tab(pos, 32)
        c64, s64 = rope_tab(pos, 64)
        C32 = np.concatenate([c32, c32], 0)
        S32 = np.concatenate([-s32, s32], 0)
        C64 = np.concatenate([c64, c64], 0)
        S64 = np.concatenate([-s64, s64], 0)
        d = dict(shared)
        d["xT"] = np.ascontiguousarray(xc.T)
        d["cvec"] = cvec
        d["ropeM"] = np.ascontiguousarray(np.concatenate([C32, S32, C32, S32], 0))
        d["ropeSC"] = np.ascontiguousarray(np.concatenate([C64, C64], 0))
        d["ropeSS"] = np.ascontiguousarray(np.concatenate([S64, S64], 0))
        per_core.append(d)
    return per_core


class Cfg:
    layers = L
    ffn1 = True
    mixer = True
    ffn2 = True
    final = True
    debug = False
    mla = True
    p2a = True
    p2b = True


STQ = "pool"


class Stream:
    def __init__(self, S, bufs, bres, loads, look):
        assert look <= len(bufs) - 1
        self.S, self.bufs, self.bres, self.loads, self.look = S, bufs, bres, loads, look
        self.nxt = 0

    def get(self, i):
        while self.nxt <= min(i + self.look, len(self.loads) - 1):
            k = self.nxt
            s = k % len(self.bufs)
            eng, mk, reads = self.loads[k]
            self.S.add(eng, mk(self.bufs[s]), reads=reads, writes=[self.bres[s]], dma=True)
            self.nxt += 1
        s = i % len(self.bufs)
        return self.bufs[s], self.bres[s]


def build(cfg):
    nc = bass.Bass("TRN2", target_bir_lowering=False)
    S = Sched()
    es = contextlib.ExitStack()
    with es:
        def dram_in(name, shape, dt=F32):
            return nc.dram_tensor(name, list(shape), dt, kind="ExternalInput").ap()

        def dram_tmp(name, shape, dt):
            return nc.dram_tensor(name, list(shape), dt).ap()

        xT = dram_in("xT", [D, T])
        yT = nc.dram_tensor("yT", [D, T], F32, kind="ExternalOutput").ap()
        dbg = nc.dram_tensor("dbg", [128, 64 * TT], F32, kind="ExternalOutput").ap() if cfg.debug else None
        WSPEC = {"gu1": ("wgu1", [L, NJ, 128, 2048]), "dn1": ("wdn1", [L, 8, 128, NJ * 128]),
                 "gu2": ("wgu2", [L, NJ, 128, 2048]), "dn2": ("wdn2", [L, 8, 128, NJ * 128]),
                 "inA": ("winA", [L, 8, 128, 2048]), "inB": ("winB", [L, 6, 128, 2048]),
                 "inG": ("winG", [L, 8, 128, 3072]), "wy": ("wy", [L, 8, 128, 1536]), "wo": ("wo", [L, 8, 128, 1024])}
        WF = {k: dram_in(v[0], v[1]) for k, v in WSPEC.items()}
        WB = {k: dram_tmp(v[0] + "_bf", v[1], BF16) for k, v in WSPEC.items()}
        wuq_d = dram_in("wuq", [L, 128, 2048])
        wukv_d = dram_in("wukv", [L, 128, 1024])
        wukv_b = dram_tmp("wukv_bf", [L, 128, 1024], BF16)
        ukvres = {}
        vecs_d = dram_in("vecs", [128, NV])
        sink_d = dram_in("sinkT", [8, L])
        masks_d = dram_in("masks", [128, 2, 128])
        sel01_d = dram_in("sel01", [8, 128])
        rsel_d = dram_in("rsel", [8, 4, 256])
        cvec_d = dram_in("cvec", [128, 66])
        ropeM_d = dram_in("ropeM", [128, T])
        ropeSC_d = dram_in("ropeSC", [128, T])
        ropeSS_d = dram_in("ropeSS", [128, T])
        Qs = dram_tmp("Qs", [8, 96, T], BF16)
        ag1s = nc.dram_tensor("ag1s", [128, T], BF16)
        ag1o = nc.dram_tensor("ag1o", [256, T], BF16)
        ag2s = nc.dram_tensor("ag2s", [64, T], BF16)
        ag2o = nc.dram_tensor("ag2o", [128, T], BF16)
        ag3s = nc.dram_tensor("ag3s", [128, 8], F32)
        ag3o = nc.dram_tensor("ag3o", [256, 8], F32)
        Ksc = dram_tmp("Ksc", [128, T + 256], BF16)
        Vsc = dram_tmp("Vsc", [T + 256, 256], BF16)
        zs = dram_tmp("zs", [4, 128, T + 2], F32)
        Os = dram_tmp("Os", [512, T], BF16)
        BYs = dram_tmp("BYs", [512, T], BF16)
        OSWs = dram_tmp("OSWs", [512, T], BF16)
        R = {n: Res(n) for n in ("Qs", "ag1s", "ag1o", "ag2s", "ag2o", "ag3s", "ag3o", "Ksc", "Vsc", "zs", "Os", "BYs", "OSWs", "Ksc_h", "Vsc_h", "zs_h")}

        uid = [0]

        def sb(name, shape, dt, stack=es):
            uid[0] += 1
            return stack.enter_context(nc.sbuf_tensor("s%d_%s" % (uid[0], name), list(shape), dt))

        X = sb("X", [128, NC8, T], F32)
        Xr = [[Res("X%d_%d" % (c, t)) for t in range(NT)] for c in range(NC8)]
        G = sb("G", [128, NV], F32)
        CV = sb("CV", [128, 66], F32)
        ones = sb("ones", [128, 128], BF16)
        epsb = sb("epsb", [128, 1], F32)
        masks = sb("masksb", [128, 2, 128], BF16)
        sel01 = sb("sel01b", [8, 128], F32)
        rsel = sb("rselb", [8, 4, 256], BF16)
        sinkT = sb("sinkTb", [8, L], F32)
        Cr = Res("consts")
        PS = es.enter_context(nc.psum_tensor("PS", [128, 8, TT], F32))
        banks = [PS[:, i, :] for i in range(8)]
        bankr = [Res("bank%d" % i) for i in range(8)]
        bstate = [0]

        def nbank():
            i = bstate[0] % 8
            bstate[0] += 1
            return PS[:, i, :], bankr[i]

        S.add("pool", lambda e: e.memset(ones[:], 1.0), writes=[Cr])
        S.add("pool", lambda e: e.memset(epsb[:], EPS), writes=[Cr])
        S.add("sp", lambda e: e.dma_start(out=G[:], in_=vecs_d), writes=[Cr], dma=True)
        S.add("sp", lambda e: e.dma_start(out=CV[:], in_=cvec_d), writes=[Cr], dma=True)
        S.add("sp", lambda e: e.dma_start(out=sel01[:], in_=sel01_d), writes=[Cr], dma=True)
        S.add("sp", lambda e: e.dma_start(out=sinkT[:], in_=sink_d), writes=[Cr], dma=True)
        S.add("pool", lambda e: e.dma_start(out=masks[:], in_=masks_d), writes=[Cr], dma=True)
        S.add("pool", lambda e: e.dma_start(out=rsel[:], in_=rsel_d), writes=[Cr], dma=True)
        for t in range(NT):
            S.add("sp", lambda e, t=t: e.dma_start(
                out=X[:, :, t * TT:(t + 1) * TT],
                in_=xT.rearrange("(c p) t -> p c t", p=128)[:, :, t * TT:(t + 1) * TT]),
                writes=[Xr[c][t] for c in range(NC8)], dma=True)
        for hh in range(2):
            S.add("sp", lambda e, hh=hh: e.dma_start(
                out=ag2s[56:64, :].rearrange("a (b c) -> (a b) c", c=256)[:, hh * 128:(hh + 1) * 128], in_=ones[:]),
                reads=[Cr], writes=[R["ag2s"]], dma=True)
        S.add("act", lambda e: e.activation(out=sinkT[:], in_=sinkT[:], func=AF.Exp), reads=[Cr], writes=[Cr])

        wres = {}

        def cast_w(nm, l):
            src, dst = WF[nm], WB[nm]
            r = Res("w%s_%d" % (nm, l))
            wres[(nm, l)] = r
            nsp = src.shape[1]
            half = nsp // 2
            for (a, b) in ((0, half), (half, nsp)):
                S.add("pool", lambda e, a=a, b=b: e.dma_start(
                    out=dst[l, a:b].rearrange("s p c -> (s p) c"),
                    in_=src[l, a:b].rearrange("s p c -> (s p) c"), max_dma_last_dim=4096),
                    writes=[r], dma=True)

        def dump(slot, ap, reads, n=1):
            if dbg is None:
                return
            S.add("pool", lambda e: e.dma_start(out=dbg[0:ap.shape[0], slot * TT:(slot + n) * TT], in_=ap), reads=reads, dma=True)

        def norm_stat(srcs, reads, nfeat, rs, rsr, P):
            bk, bkr = nbank()
            n = len(srcs)
            for c in range(n):
                q, qr = P["sq"][c % 2], P["sqr"][c % 2]
                S.add("act", lambda e, c=c, q=q: e.activation(out=q[:], in_=srcs[c], func=AF.Square),
                      reads=[reads[c]], writes=[qr])
                S.add("pe", lambda e, c=c, q=q: e.matmul(bk, lhsT=ones[:], rhs=q[:], start=(c == 0), stop=(c == n - 1)),
                      reads=[qr, Cr], writes=[bkr])
            S.add("act", lambda e: e.activation(out=rs[:], in_=bk, func=AF.Ln, bias=epsb[:], scale=1.0 / nfeat),
                  reads=[bkr, Cr], writes=[rsr])
            S.add("act", lambda e: e.activation(out=rs[:], in_=rs[:], func=AF.Exp, scale=-0.5), reads=[rsr], writes=[rsr])

        def rmsnorm_stat(t, P, rs, rsr):
            cols = slice(t * TT, (t + 1) * TT)
            norm_stat([X[:, c, cols] for c in range(NC8)], [Xr[c][t] for c in range(NC8)], D, rs, rsr, P)

        def rmsnorm_apply(t, gcol, P, rs, rsr):
            cols = slice(t * TT, (t + 1) * TT)
            for c in range(NC8):
                S.add("dve", lambda e, c=c: e.scalar_tensor_tensor(
                    out=P["xn"][:, c, :], in0=X[:, c, cols], scalar=G[:, gcol + c:gcol + c + 1], in1=rs[:],
                    op0=ALU.mult, op1=ALU.mult),
                    reads=[Xr[c][t], Cr, rsr], writes=[P["xnr"][c]])

        def rmsnorm_tile(t, gcol, P):
            rmsnorm_stat(t, P, P["rstd"], P["rstdr"])
            rmsnorm_apply(t, gcol, P, P["rstd"], P["rstdr"])

        PG = {}
        PG["xn"] = sb("xn", [128, NC8, TT], BF16)
        PG["xnr"] = [Res("xn%d" % c) for c in range(NC8)]
        pre_done = [None]

        def norm_bufs(ps, P):
            P.update(PG)
            P["sq"] = [sb("sq%d" % i, [128, TT], BF16, ps) for i in range(2)]
            P["sqr"] = [Res("sq%d" % i) for i in range(2)]
            P["rstd"] = sb("rstd", [128, TT], F32, ps)
            P["rstdr"] = Res("rstd")

        def first_norm(t, gcol, P):
            if pre_done[0] == (t, gcol):
                pre_done[0] = None
                return
            assert pre_done[0] is None, pre_done[0]
            rmsnorm_tile(t, gcol, P)

        def pre_norm(nxt, P):
            if nxt is not None:
                rmsnorm_tile(nxt[0], nxt[1], P)
                pre_done[0] = nxt

        def slab_stream(ps, name, nm, l, nslab_per_tile, shape, nbuf, look, ntiles=NT, jorder=None):
            bufs = [sb("%s%d" % (name, i), shape, BF16, ps) for i in range(nbuf)]
            bres = [Res("%s%d" % (name, i)) for i in range(nbuf)]
            loads = []
            k = shape[1]
            for t in range(ntiles):
                for j in (jorder or range(nslab_per_tile)):
                    loads.append(("sp", (lambda buf, j=j: (lambda e: e.dma_start(
                        out=buf[:], in_=WB[nm][l, j].rearrange("p (k c) -> p k c", k=k)))), [wres[(nm, l)]]))
            return Stream(S, bufs, bres, loads, look)

        def ffn_phase(l, which, nxt=None):
            nm_gu, nm_dn = ("gu1", "dn1") if which == 1 else ("gu2", "dn2")
            gcol = (G_F1 if which == 1 else G_F2) + l * 8
            with contextlib.ExitStack() as ps:
                P = {}
                norm_bufs(ps, P)
                hid = sb("hid", [128, NJ, TT], BF16, ps)
                hidr = [Res("hid%d" % j) for j in range(NJ)]
                NSA = 3
                sa = [sb("sa%d" % i, [128, TT], BF16, ps) for i in range(NSA)]
                sar = [Res("sa%d" % i) for i in range(NSA)]
                gst = slab_stream(ps, "gub", nm_gu, l, NJ, [128, 8, 256], 4, 3)
                dst_ = slab_stream(ps, "dnb", nm_dn, l, 8, [128, NJ, 128], 3, 2)
                gst.get(0)
                first_norm(0, gcol, P)
                for t in range(NT):
                    cols = slice(t * TT, (t + 1) * TT)
                    for j in range(NJ):
                        gb, gbr = gst.get(t * NJ + j)
                        if j == NJ - 3:
                            dst_.get(t * 8)
                        ba, bar_ = nbank()
                        bb, bbr = nbank()
                        for k in range(8):
                            S.add("pe", lambda e, gb=gb, k=k, ba=ba: e.matmul(
                                ba, lhsT=gb[:, k, 0:128], rhs=P["xn"][:, k, :], start=(k == 0), stop=(k == 7)),
                                reads=[gbr, P["xnr"][k]], writes=[bar_])
                        for k in range(8):
                            S.add("pe", lambda e, gb=gb, k=k, bb=bb: e.matmul(
                                bb, lhsT=gb[:, k, 128:256], rhs=P["xn"][:, k, :], start=(k == 0), stop=(k == 7)),
                                reads=[gbr, P["xnr"][k]], writes=[bbr])
                        si = j % NSA
                        S.add("act", lambda e, si=si, ba=ba: e.activation(out=sa[si][:], in_=ba, func=AF.Silu),
                              reads=[bar_], writes=[sar[si]])
                        S.add("dve", lambda e, si=si, bb=bb, j=j: e.tensor_tensor(
                            out=hid[:, j, :], in0=bb, in1=sa[si][:], op=ALU.mult),
                            reads=[bbr, sar[si]], writes=[hidr[j]])
                    if t + 1 < NT:
                        rmsnorm_tile(t + 1, gcol, P)
                    else:
                        pre_norm(nxt, P)
                    for m in range(8):
                        db, dbr = dst_.get(t * 8 + m)
                        bo, bor = nbank()
                        for j in range(NJ):
                            S.add("pe", lambda e, db=db, j=j, bo=bo: e.matmul(
                                bo, lhsT=db[:, j, :], rhs=hid[:, j, :], start=(j == 0), stop=(j == NJ - 1)),
                                reads=[dbr, hidr[j]], writes=[bor])
                        S.add("dve", lambda e, m=m, bo=bo, cols=cols: e.scalar_tensor_tensor(
                            out=X[:, m, cols], in0=bo, scalar=0.5, in1=X[:, m, cols],
                            op0=ALU.mult, op1=ALU.add),
                            reads=[bor, Xr[m][t]], writes=[Xr[m][t]])
                S.barrier()

        def pass1(l, nxt=None):
            with contextlib.ExitStack() as ps:
                P = {}
                norm_bufs(ps, P)
                gcol = G_MIX + l * 8
                st = slab_stream(ps, "wa", "inA", l, 8, [128, 8, 256], 3, 2)
                wuq = sb("wuq", [128, 2, 8, 128], BF16, ps)
                wuqr = Res("wuq")
                S.add("pool", lambda e: e.dma_start(out=wuq[:], in_=wuq_d[l].rearrange("p (k h c) -> p k h c", k=2, h=8)),
                      writes=[wuqr], dma=True)
                rq = sb("rq", [128, TT], F32, ps); rqr = Res("rq")
                rs2 = sb("rstd2", [128, TT], F32, ps)
                rsl = [(P["rstd"], P["rstdr"]), (rs2, Res("rstd2"))]
                qn = sb("qn", [128, 2, TT], BF16, ps); qnr = Res("qn")
                kvst = sb("kvst", [128, TT], BF16, ps); kvstr = Res("kvst")
                krst = sb("krst", [32, TT], BF16, ps); krstr = Res("krst")
                NTMP = 5
                tmp = [sb("tmp%d" % i, [128, TT], F32, ps) for i in range(NTMP)]
                tmpr = [Res("tmp%d" % i) for i in range(NTMP)]
                csb = sb("csb", [128, 4, TT], F32, ps); csbr = [Res("csb%d" % i) for i in range(4)]
                zst = [sb("zst%d" % i, [128, TT], F32, ps) for i in range(3)]
                zstr = [Res("zst%d" % i) for i in range(3)]
                kst = [sb("kst%d" % i, [128, TT], BF16, ps) for i in range(2)]
                kstr = [Res("kst%d" % i) for i in range(2)]
                vst = [sb("vst%d" % i, [128, 4, 2, 128], BF16, ps) for i in range(2)]
                vstr = [Res("vst%d" % i) for i in range(2)]
                qst = [sb("qst%d" % i, [96, TT], BF16, ps) for i in range(3)]
                qstr = [Res("qst%d" % i) for i in range(3)]
                rM = sb("rM", [128, TT], F32, ps); rMr = Res("rM")
                rC = sb("rC", [128, TT], F32, ps); rCr = Res("rC")
                rS = sb("rS", [128, TT], F32, ps); rSr = Res("rS")
                for i in range(2):
                    S.add("pool", lambda e, i=i: e.memset(vst[i][:], 1.0), writes=[vstr[i]])
                tcnt = [0]

                def ntmp():
                    i = tcnt[0] % NTMP
                    tcnt[0] += 1
                    return tmp[i], tmpr[i]

                zc = [0]
                qc = [0]
                st.get(0)
                first_norm(0, gcol, P)
                for t in range(NT):
                    cols = slice(t * TT, (t + 1) * TT)
                    st.get(t * 8)
                    S.add("sp", lambda e, cols=cols: e.dma_start(out=rM[:], in_=ropeM_d[:, cols]), writes=[rMr], dma=True)
                    S.add("sp", lambda e, cols=cols: e.dma_start(out=rC[:], in_=ropeSC_d[:, cols]), writes=[rCr], dma=True)
                    S.add("sp", lambda e, cols=cols: e.dma_start(out=rS[:], in_=ropeSS_d[:, cols]), writes=[rSr], dma=True)

                    def group(wb, wbr, c0, c1, bk, bkr, rows=None):
                        for k in range(8):
                            S.add("pe", lambda e, k=k: e.matmul(
                                bk if rows is None else bk[0:rows, :], lhsT=wb[:, k, c0:c1], rhs=P["xn"][:, k, :],
                                start=(k == 0), stop=(k == 7)), reads=[wbr, P["xnr"][k]], writes=[bkr])

                    def q_heads(hs, cols=cols):
                        for h in hs:
                            bk, bkr = nbank()
                            for k in range(2):
                                S.add("pe", lambda e, k=k, h=h, bk=bk: e.matmul(bk, lhsT=wuq[:, k, h, :], rhs=qn[:, k, :], start=(k == 0), stop=(k == 1)),
                                      reads=[wuqr, qnr], writes=[bkr])
                            qi = qc[0] % 3
                            qc[0] += 1
                            t1, t1r = ntmp()
                            t2, t2r = ntmp()
                            S.add("act", lambda e, bk=bk, qi=qi: e.activation(out=qst[qi][0:64, :], in_=bk[0:64, :], func=AF.Copy),
                                  reads=[bkr], writes=[qstr[qi]])
                            S.add("dve", lambda e, bk=bk, t1=t1: e.tensor_tensor(out=t1[64:96, :], in0=bk[64:96, :], in1=rM[64:96, :], op=ALU.mult),
                                  reads=[bkr, rMr], writes=[t1r])
                            S.add("dve", lambda e, bk=bk, t2=t2: e.tensor_tensor(out=t2[64:96, :], in0=bk[96:128, :], in1=rM[96:128, :], op=ALU.mult),
                                  reads=[bkr, rMr], writes=[t2r])
                            S.add("dve", lambda e, t1=t1, t2=t2, qi=qi: e.tensor_tensor(out=qst[qi][64:96, :], in0=t1[64:96, :], in1=t2[64:96, :], op=ALU.add),
                                  reads=[t1r, t2r, qstr[qi]], writes=[qstr[qi]])
                            S.add(STQ, lambda e, h=h, qi=qi, cols=cols: e.dma_start(out=Qs[h, :, cols], in_=qst[qi][:]),
                                  reads=[qstr[qi]], writes=[R["Qs"]], dma=True)

                    wb, wbr = st.get(t * 8 + 0)
                    bq = [nbank(), nbank()]
                    for c in range(2):
                        group(wb, wbr, c * 128, (c + 1) * 128, bq[c][0], bq[c][1])
                    norm_stat([bq[0][0], bq[1][0]], [bq[0][1], bq[1][1]], 256, rq, rqr, P)
                    for c in range(2):
                        S.add("dve", lambda e, c=c, bq=bq: e.scalar_tensor_tensor(
                            out=qn[:, c, :], in0=bq[c][0], scalar=G[:, G_Q + l * 2 + c:G_Q + l * 2 + c + 1], in1=rq[:],
                            op0=ALU.mult, op1=ALU.mult), reads=[bq[c][1], Cr, rqr], writes=[qnr])
                    wb, wbr = st.get(t * 8 + 1)
                    bkv, bkvr = nbank()
                    group(wb, wbr, 0, 128, bkv, bkvr)
                    bkr_, bkrr = nbank()
                    group(wb, wbr, 128, 192, bkr_, bkrr, rows=64)
                    norm_stat([bkv], [bkvr], 128, rq, rqr, P)
                    S.add("dve", lambda e, bkv=bkv: e.scalar_tensor_tensor(
                        out=kvst[:], in0=bkv, scalar=G[:, G_KV + l:G_KV + l + 1], in1=rq[:],
                        op0=ALU.mult, op1=ALU.mult), reads=[bkvr, Cr, rqr], writes=[kvstr])
                    S.add(STQ, lambda e, cols=cols: e.dma_start(out=ag1s[:, cols], in_=kvst[:]), reads=[kvstr], writes=[R["ag1s"]], dma=True)
                    t1, t1r = ntmp()
                    t2, t2r = ntmp()
                    S.add("dve", lambda e, t1=t1, bkr_=bkr_: e.tensor_tensor(out=t1[0:32, :], in0=bkr_[0:32, :], in1=rM[0:32, :], op=ALU.mult),
                          reads=[bkrr, rMr], writes=[t1r])
                    S.add("dve", lambda e, t2=t2, bkr_=bkr_: e.tensor_tensor(out=t2[0:32, :], in0=bkr_[32:64, :], in1=rM[32:64, :], op=ALU.mult),
                          reads=[bkrr, rMr], writes=[t2r])
                    S.add("dve", lambda e, t1=t1, t2=t2: e.tensor_tensor(out=krst[:], in0=t1[0:32, :], in1=t2[0:32, :], op=ALU.add),
                          reads=[t1r, t2r], writes=[krstr])
                    S.add(STQ, lambda e, cols=cols: e.dma_start(out=ag2s[0:32, cols], in_=krst[:]), reads=[krstr], writes=[R["ag2s"]], dma=True)
                    q_heads(range(0, 4))
                    for hf in range(2):
                        wb, wbr = st.get(t * 8 + 2 + hf)
                        for c in range(2):
                            bk, bkr = nbank()
                            group(wb, wbr, c * 128, (c + 1) * 128, bk, bkr)
                            ci = hf * 2 + c
                            S.add("act", lambda e, ci=ci, bk=bk: e.activation(out=csb[:, ci, :], in_=bk, func=AF.Copy),
                                  reads=[bkr], writes=[csbr[ci]])
                    if t + 1 < NT:
                        rmsnorm_stat(t + 1, P, *rsl[(t + 1) % 2])
                    for hf in range(2):
                        wb, wbr = st.get(t * 8 + 4 + hf)
                        for c in range(2):
                            bk, bkr = nbank()
                            group(wb, wbr, c * 128, (c + 1) * 128, bk, bkr)
                            ci = hf * 2 + c
                            zi = zc[0] % 3
                            zc[0] += 1
                            S.add("dve", lambda e, ci=ci, bk=bk, zi=zi: e.tensor_tensor(out=zst[zi][:], in0=bk, in1=csb[:, ci, :], op=ALU.mult),
                                  reads=[bkr, csbr[ci]], writes=[zstr[zi]])
                            S.add(STQ, lambda e, ci=ci, zi=zi, t=t: e.dma_start(out=zs[ci, :, 1 + t * TT:1 + (t + 1) * TT], in_=zst[zi][:]),
                                  reads=[zstr[zi]], writes=[R["zs"]], dma=True)
                    wb, wbr = st.get(t * 8 + 6)
                    bA, bAr = nbank()
                    group(wb, wbr, 0, 128, bA, bAr)
                    bB, bBr = nbank()
                    group(wb, wbr, 128, 256, bB, bBr)
                    t1, t1r = ntmp()
                    t2, t2r = ntmp()
                    ki = t % 2
                    S.add("dve", lambda e, t1=t1, bA=bA: e.tensor_tensor(out=t1[:], in0=bA, in1=rC[:], op=ALU.mult), reads=[bAr, rCr], writes=[t1r])
                    S.add("dve", lambda e, t2=t2, bB=bB: e.tensor_tensor(out=t2[:], in0=bB, in1=rS[:], op=ALU.mult), reads=[bBr, rSr], writes=[t2r])
                    S.add("dve", lambda e, t1=t1, t2=t2, ki=ki: e.tensor_tensor(out=kst[ki][:], in0=t1[:], in1=t2[:], op=ALU.add),
                          reads=[t1r, t2r], writes=[kstr[ki]])
                    S.add(STQ, lambda e, ki=ki, t=t: e.dma_start(out=Ksc[:, 128 + t * TT:128 + (t + 1) * TT], in_=kst[ki][:]),
                          reads=[kstr[ki]], writes=[R["Ksc"]], dma=True)
                    wb, wbr = st.get(t * 8 + 7)
                    bk, bkr = nbank()
                    for blk in range(4):
                        for k in range(8):
                            S.add("pe", lambda e, k=k, blk=blk, bk=bk, wb=wb: e.matmul(
                                bk[:, blk * 128:(blk + 1) * 128], lhsT=P["xn"][:, k, blk * 128:(blk + 1) * 128], rhs=wb[:, k, 0:128],
                                start=(k == 0), stop=(k == 7)), reads=[wbr, P["xnr"][k]], writes=[bkr])
                    vi = t % 2
                    S.add("act", lambda e, bk=bk, vi=vi: e.activation(
                        out=vst[vi][:, :, :, 0:64], in_=bk.rearrange("p (b g c) -> p b g c", b=4, g=2), func=AF.Copy),
                        reads=[bkr], writes=[vstr[vi]])
                    S.add(STQ, lambda e, vi=vi, t=t: e.dma_start(
                        out=Vsc[128 + t * TT:128 + (t + 1) * TT, :].rearrange("(b p) c -> p b c", p=128),
                        in_=vst[vi][:].rearrange("p b g c -> p b (g c)")), reads=[vstr[vi]], writes=[R["Vsc"]], dma=True)
                    if t + 1 < NT:
                        rmsnorm_apply(t + 1, gcol, P, *rsl[(t + 1) % 2])
                    else:
                        pre_norm(nxt, P)
                    q_heads(range(4, 8))
                S.barrier()

        def exchange_start(l):
            K4 = lambda r0: ag2s[r0:r0 + 4, :].rearrange("a (b t) -> (a b) t", t=128)
            V8 = lambda r0: ag2s[r0:r0 + 8, :].rearrange("a (b c) -> (a b) c", c=256)
            S.add(STQ, lambda e: e.dma_start(out=K4(32), in_=Ksc[:, 128:256]), reads=[R["Ksc"]], writes=[R["ag2s"]], dma=True)
            S.add(STQ, lambda e: e.dma_start(out=K4(36), in_=Ksc[:, T:T + 128]), reads=[R["Ksc"]], writes=[R["ag2s"]], dma=True)
            S.add(STQ, lambda e: e.dma_start(out=V8(40), in_=Vsc[128:256, :]), reads=[R["Vsc"]], writes=[R["ag2s"]], dma=True)
            S.add(STQ, lambda e: e.dma_start(out=V8(48), in_=Vsc[T:T + 128, :]), reads=[R["Vsc"]], writes=[R["ag2s"]], dma=True)
            for ci in range(4):
                S.add(STQ, lambda e, ci=ci: e.dma_start(out=ag3s[:, 2 * ci:2 * ci + 1], in_=zs[ci, :, 1:2], allow_slow_non_contiguous=True),
                      reads=[R["zs"]], writes=[R["ag3s"]], dma=True)
                S.add(STQ, lambda e, ci=ci: e.dma_start(out=ag3s[:, 2 * ci + 1:2 * ci + 2], in_=zs[ci, :, T:T + 1], allow_slow_non_contiguous=True),
                      reads=[R["zs"]], writes=[R["ag3s"]], dma=True)
            GR = [[0, 1], [2, 3], [4, 5], [6, 7]]
            for (a, b_, rs_, ro) in ((ag1s, ag1o, "ag1s", "ag1o"), (ag2s, ag2o, "ag2s", "ag2o"), (ag3s, ag3o, "ag3s", "ag3o")):
                S.add("pool", lambda e, a=a, b_=b_: e.collective_compute(
                    "AllGather", ALU.bypass, replica_groups=GR, ins=[a.ap().opt()], outs=[b_.ap().opt()]),
                    reads=[R[rs_]], writes=[R[ro]], cc=True)

        def exchange_finish(l, ps):
            K4o = lambda r0: ag2o[r0:r0 + 4, :].rearrange("a (b t) -> (a b) t", t=128)
            V8o = lambda r0: ag2o[r0:r0 + 8, :].rearrange("a (b c) -> (a b) c", c=256)
            S.add("sp", lambda e: e.dma_start(out=Ksc[:, 0:128], in_=K4o(36)), reads=[R["ag2o"]], writes=[R["Ksc_h"]], dma=True)
            S.add("sp", lambda e: e.dma_start(out=Ksc[:, T + 128:T + 256], in_=K4o(64 + 32)), reads=[R["ag2o"]], writes=[R["Ksc_h"]], dma=True)
            vh = sb("vh", [128, 2, 256], BF16, ps); vhr = Res("vh")
            zh = sb("zh", [128, 2, 8], F32, ps); zhr = Res("zh")
            S.add("sp", lambda e: e.dma_start(out=vh[:, 0, :], in_=V8o(48)), reads=[R["ag2o"]], writes=[vhr], dma=True)
            S.add("sp", lambda e: e.dma_start(out=vh[:, 1, :], in_=V8o(64 + 40)), reads=[R["ag2o"]], writes=[vhr], dma=True)
            S.add("sp", lambda e: e.dma_start(out=zh[:, 0, :], in_=ag3o[0:128, :]), reads=[R["ag3o"]], writes=[zhr], dma=True)
            S.add("sp", lambda e: e.dma_start(out=zh[:, 1, :], in_=ag3o[128:256, :]), reads=[R["ag3o"]], writes=[zhr], dma=True)
            for i in range(2):
                S.add("dve", lambda e, i=i: e.tensor_scalar(out=vh[:, i, :], in0=vh[:, i, :], scalar1=CV[:, 64 + i:65 + i], scalar2=None, op0=ALU.mult),
                      reads=[vhr, Cr], writes=[vhr])
                S.add("dve", lambda e, i=i: e.tensor_scalar(out=zh[:, i, :], in0=zh[:, i, :], scalar1=CV[:, 64 + i:65 + i], scalar2=None, op0=ALU.mult),
                      reads=[zhr, Cr], writes=[zhr])
            S.add("sp", lambda e: e.dma_start(out=Vsc[0:128, :], in_=vh[:, 0, :]), reads=[vhr], writes=[R["Vsc_h"]], dma=True)
            S.add("sp", lambda e: e.dma_start(out=Vsc[T + 128:T + 256, :], in_=vh[:, 1, :]), reads=[vhr], writes=[R["Vsc_h"]], dma=True)
            for ci in range(4):
                S.add("sp", lambda e, ci=ci: e.dma_start(out=zs[ci, :, 0:1], in_=zh[:, 0, 2 * ci + 1:2 * ci + 2], allow_slow_non_contiguous=True), reads=[zhr], writes=[R["zs_h"]], dma=True)
                S.add("sp", lambda e, ci=ci: e.dma_start(out=zs[ci, :, T + 1:T + 2], in_=zh[:, 1, 2 * ci:2 * ci + 1], allow_slow_non_contiguous=True), reads=[zhr], writes=[R["zs_h"]], dma=True)

        def mla_phase(l):
            with contextlib.ExitStack() as ps:
                KVN = sb("KVN", [128, 2 * T], BF16, ps); KVNr = Res("KVN")
                KhT = sb("KhT", [96, 2 * T], BF16, ps); KhTr = Res("KhT"); KhTe = [Res("KhTe0"), Res("KhTe1")]
                VA = sb("VA", [128, 64, 128], BF16, ps); VAr = Res("VA")
                QhT = sb("QhT", [96, T], BF16, ps); QhTr = Res("QhT")
                wukv = sb("wukv", [128, 1024], BF16, ps); wukvr = Res("wukv")
                NPB = 3
                Pb = [sb("Pb%d" % i, [128, 2, TT], BF16, ps) for i in range(NPB)]
                Pbr = [Res("Pb%d" % i) for i in range(NPB)]
                rsb = [sb("rsb%d" % i, [64, TT], F32, ps) for i in range(1)] * 2
                rsbr = [Res("rsb%d" % i) for i in range(1)] * 2
                ost = [sb("ost%d" % i, [64, TT], BF16, ps) for i in range(1)] * 2
                ostr = [Res("ost%d" % i) for i in range(1)] * 2
                S.add("sp", lambda e: e.dma_start(out=wukv[:], in_=wukv_b[l]), reads=[ukvres[l]], writes=[wukvr], dma=True)
                for r in range(2):
                    S.add("sp", lambda e, r=r: e.dma_start(out=KVN[:, r * T:(r + 1) * T], in_=ag1o[r * 128:(r + 1) * 128, :]),
                          reads=[R["ag1o"]], writes=[KVNr], dma=True)
                    S.add("sp", lambda e, r=r: e.dma_start(out=KhT[64:96, r * T:(r + 1) * T], in_=ag2o[r * 64:r * 64 + 32, :]),
                          reads=[R["ag2o"]], writes=[KhTr], dma=True)
                for kc in range(64):
                    S.add("dve", lambda e, kc=kc: e.tensor_scalar(out=VA[:, kc, 64:128], in0=ones[:, 0:64], scalar1=CV[:, kc:kc + 1], scalar2=None, op0=ALU.mult),
                          reads=[Cr], writes=[VAr])
                SC = float(96 ** -0.5)
                pcnt = [0]
                ocnt = [0]
                sbank = [0]
                for h in range(8):
                    S.add("sp", lambda e, h=h: e.dma_start(out=QhT[:], in_=Qs[h]), reads=[R["Qs"]], writes=[QhTr], dma=True)
                    for kt in range(16):
                        bk, bkr = nbank()
                        S.add("pe", lambda e, kt=kt, h=h, bk=bk: e.matmul(bk[0:64, :], lhsT=wukv[:, h * 128:h * 128 + 64], rhs=KVN[:, kt * TT:(kt + 1) * TT],
                                                                   start=True, stop=True), reads=[wukvr, KVNr], writes=[bkr])
                        if kt % 2 == 0:
                            S.add("act", lambda e, kt=kt, bk=bk: e.activation(out=KhT[0:64, kt * TT:(kt + 1) * TT], in_=bk[0:64, :], func=AF.Copy),
                                  reads=[bkr], writes=[KhTe[0]])
                        else:
                            S.add("dve", lambda e, kt=kt, bk=bk: e.tensor_copy(out=KhT[0:64, kt * TT:(kt + 1) * TT], in_=bk[0:64, :]),
                                  reads=[bkr], writes=[KhTe[1]])
                    for kt in range(16):
                        bk, bkr = nbank()
                        for j in range(4):
                            kc = kt * 4 + j
                            S.add("pe", lambda e, kc=kc, j=j, h=h, bk=bk: e.matmul(bk[:, j * 64:(j + 1) * 64], lhsT=KVN[:, kc * 128:(kc + 1) * 128],
                                                                             rhs=wukv[:, h * 128 + 64:h * 128 + 128], start=True, stop=True),
                                  reads=[wukvr, KVNr], writes=[bkr])
                        S.add("dve", lambda e, kt=kt, bk=bk: e.tensor_tensor(
                            out=VA[:, kt * 4:kt * 4 + 4, 0:64], in0=bk[:, 0:256].rearrange("p (j c) -> p j c", j=4),
                            in1=CV[:, kt * 4:kt * 4 + 4].unsqueeze(2).broadcast_to([128, 4, 64]), op=ALU.mult),
                            reads=[bkr, Cr], writes=[VAr])
                    for qt in range(NT):
                        qcols = slice(qt * TT, (qt + 1) * TT)
                        oi = ocnt[0] % 2
                        ocnt[0] += 1
                        ob, obr = PS[:, oi, :], bankr[oi]
                        SK = 2
                        pend = []
                        for kp in range(32 + SK):
                            if kp < 32:
                                pr = sbank[0] % 3
                                sbank[0] += 1
                                b0 = 2 + 2 * pr
                                for j in range(2):
                                    kc = 2 * kp + j
                                    S.add("pe", lambda e, kc=kc, bj=b0 + j, qcols=qcols: e.matmul(PS[:, bj, :], lhsT=KhT[:, kc * 128:(kc + 1) * 128], rhs=QhT[:, qcols],
                                                                                           start=True, stop=True), reads=[KhTr, KhTe[0], KhTe[1], QhTr], writes=[bankr[b0 + j]])
                                pi = pcnt[0] % NPB
                                pcnt[0] += 1
                                S.add("act", lambda e, b0=b0, pi=pi: e.activation(out=Pb[pi][:], in_=PS[:, b0:b0 + 2, :], func=AF.Exp, scale=SC),
                                      reads=[bankr[b0], bankr[b0 + 1]], writes=[Pbr[pi]])
                                pend.append(pi)
                            if kp >= SK:
                                k2 = kp - SK
                                pi = pend[k2]
                                for j in range(2):
                                    kc = 2 * k2 + j
                                    S.add("pe", lambda e, kc=kc, j=j, pi=pi, ob=ob: e.matmul(ob, lhsT=VA[:, kc, :], rhs=Pb[pi][:, j, :], start=(kc == 0), stop=(kc == 63)),
                                          reads=[VAr, Pbr[pi]], writes=[obr])
                        S.add("dve", lambda e, ob=ob, oi=oi: e.reciprocal(out=rsb[oi][:], in_=ob[64:128, :]), reads=[obr], writes=[rsbr[oi]])
                        S.add("dve", lambda e, ob=ob, oi=oi: e.tensor_tensor(out=ost[oi][:], in0=ob[0:64, :], in1=rsb[oi][:], op=ALU.mult),
                              reads=[obr, rsbr[oi]], writes=[ostr[oi]])
                        S.add("sp", lambda e, oi=oi, h=h, qcols=qcols: e.dma_start(out=Os[h * 64:(h + 1) * 64, qcols], in_=ost[oi][:]),
                              reads=[ostr[oi]], writes=[R["Os"]], dma=True)
                S.barrier()

        def pass2a(l, nxt=None):
            with contextlib.ExitStack() as ps:
                P = {}
                norm_bufs(ps, P)
                gcol = G_MIX + l * 8
                st = slab_stream(ps, "wb", "inB", l, 6, [128, 8, 256], 3, 2, jorder=[2, 3, 4, 5, 0, 1])
                ZW = [sb("ZW%d" % i, [128, TT + 2], F32, ps) for i in range(2)]
                ZWr = [Res("ZW%d" % i) for i in range(2)]
                ycb = [sb("ycb%d" % i, [128, TT], F32, ps) for i in range(2)]
                ycbr = [Res("ycb%d" % i) for i in range(2)]
                byst = sb("byst", [128, 4, TT], BF16, ps); bystr = Res("byst")
                Qsw = sb("Qsw", [128, 4, TT], BF16, ps); Qswr = Res("Qsw")
                Osw = sb("Osw", [128, 4, TT], BF16, ps); Oswr = Res("Osw")
                KW = sb("KW", [128, 2, 768], BF16, ps); KWr = Res("KW")
                VW = sb("VW", [128, 6, 256], BF16, ps); VWr = Res("VW")
                NPB = 12
                Pb = [sb("Pw%d" % i, [128, 256], BF16, ps) for i in range(NPB)]
                Pbr = [Res("Pw%d" % i) for i in range(NPB)]
                rC = sb("rC", [128, TT], F32, ps); rCr = Res("rC")
                rS = sb("rS", [128, TT], F32, ps); rSr = Res("rS")
                tmp = [sb("tmp%d" % i, [128, TT], F32, ps) for i in range(4)]
                tmpr = [Res("tmp%d" % i) for i in range(4)]
                rsb = [sb("rsw%d" % i, [128, 256], F32, ps) for i in range(4)]
                rsbr = [Res("rsw%d" % i) for i in range(4)]
                Esk = sb("Esk", [8, 128], BF16, ps); Eskr = Res("Esk")
                rs2 = sb("rstd2a", [128, TT], F32, ps)
                rsl = [(P["rstd"], P["rstdr"]), (rs2, Res("rstd2a"))]
                S.add("dve", lambda e: e.tensor_scalar(out=Esk[:], in0=sel01[:], scalar1=sinkT[:, l:l + 1], scalar2=None, op0=ALU.mult),
                      reads=[Cr], writes=[Eskr])
                cwc = G_CW + l * 12
                pcnt = [0]
                rcnt = [0]
                zcnt = [0]
                order = [1, 2, 3, 4, 5, 6, 0, 7]
                for ti, t in enumerate(order):
                    cols = slice(t * TT, (t + 1) * TT)
                    if ti == 6:
                        exchange_finish(l, ps)
                    edge = t in (0, NT - 1)
                    st.get(ti * 6)
                    S.add("sp", lambda e, cols=cols: e.dma_start(out=rC[:], in_=ropeSC_d[:, cols]), writes=[rCr], dma=True)
                    S.add("sp", lambda e, cols=cols: e.dma_start(out=rS[:], in_=ropeSS_d[:, cols]), writes=[rSr], dma=True)
                    for dup in range(2):
                        for g in range(2):
                            S.add("sp", lambda e, dup=dup, g=g, t=t: e.dma_start(out=KW[dup * 64:(dup + 1) * 64, g, :], in_=Ksc[g * 64:(g + 1) * 64, t * TT:t * TT + 768]),
                                  reads=[R["Ksc"]] + ([R["Ksc_h"]] if edge else []), writes=[KWr], dma=True)
                    S.add("sp", lambda e, t=t: e.dma_start(out=VW[:], in_=Vsc[t * TT:t * TT + 768, :].rearrange("(b p) c -> p b c", p=128)),
                          reads=[R["Vsc"]] + ([R["Vsc_h"]] if edge else []), writes=[VWr], dma=True)
                    if ti == 0:
                        first_norm(t, gcol, P)

                    def group(wb, wbr, c0, c1, bk, bkr):
                        for k in range(8):
                            S.add("pe", lambda e, k=k: e.matmul(bk, lhsT=wb[:, k, c0:c1], rhs=P["xn"][:, k, :], start=(k == 0), stop=(k == 7)),
                                  reads=[wbr, P["xnr"][k]], writes=[bkr])

                    for i in range(4):
                        wb, wbr = st.get(ti * 6 + i)
                        bA, bAr = nbank()
                        group(wb, wbr, 0, 128, bA, bAr)
                        bB, bBr = nbank()
                        group(wb, wbr, 128, 256, bB, bBr)
                        ta, tb = 2 * (i % 2), 2 * (i % 2) + 1
                        S.add("dve", lambda e, bA=bA, ta=ta: e.tensor_tensor(out=tmp[ta][:], in0=bA, in1=rC[:], op=ALU.mult), reads=[bAr, rCr], writes=[tmpr[ta]])
                        S.add("dve", lambda e, bB=bB, tb=tb: e.tensor_tensor(out=tmp[tb][:], in0=bB, in1=rS[:], op=ALU.mult), reads=[bBr, rSr], writes=[tmpr[tb]])
                        S.add("dve", lambda e, i=i, ta=ta, tb=tb: e.tensor_tensor(out=Qsw[:, i, :], in0=tmp[ta][:], in1=tmp[tb][:], op=ALU.add),
                              reads=[tmpr[ta], tmpr[tb]], writes=[Qswr])
                    if ti + 1 < NT:
                        rmsnorm_stat(order[ti + 1], P, *rsl[(ti + 1) % 2])
                    for hf in range(2):
                        wb, wbr = st.get(ti * 6 + 4 + hf)
                        for c in range(2):
                            ci = hf * 2 + c
                            bk, bkr = nbank()
                            group(wb, wbr, c * 128, (c + 1) * 128, bk, bkr)
                            zi = zcnt[0] % 2
                            zcnt[0] += 1
                            S.add("sp", lambda e, ci=ci, zi=zi, t=t: e.dma_start(out=ZW[zi][:], in_=zs[ci, :, t * TT:t * TT + TT + 2]),
                                  reads=[R["zs"]] + ([R["zs_h"]] if edge else []), writes=[ZWr[zi]], dma=True)
                            yc, ycr = ycb[zi], ycbr[zi]
                            w0 = cwc + ci * 3
                            S.add("dve", lambda e, zi=zi, yc=yc, w0=w0: e.tensor_scalar(out=yc[:], in0=ZW[zi][:, 0:TT], scalar1=G[:, w0:w0 + 1], scalar2=None, op0=ALU.mult),
                                  reads=[ZWr[zi], Cr], writes=[ycr])
                            S.add("dve", lambda e, zi=zi, yc=yc, w0=w0: e.scalar_tensor_tensor(out=yc[:], in0=ZW[zi][:, 1:TT + 1], scalar=G[:, w0 + 1:w0 + 2], in1=yc[:],
                                                                                          op0=ALU.mult, op1=ALU.add), reads=[ZWr[zi], Cr, ycr], writes=[ycr])
                            S.add("dve", lambda e, zi=zi, yc=yc, w0=w0: e.scalar_tensor_tensor(out=yc[:], in0=ZW[zi][:, 2:TT + 2], scalar=G[:, w0 + 2:w0 + 3], in1=yc[:],
                                                                                          op0=ALU.mult, op1=ALU.add), reads=[ZWr[zi], Cr, ycr], writes=[ycr])
                            S.add("dve", lambda e, ci=ci, bk=bk, yc=yc: e.tensor_tensor(out=byst[:, ci, :], in0=bk, in1=yc[:], op=ALU.mult),
                                  reads=[bkr, ycr], writes=[bystr])
                    S.add(STQ, lambda e, cols=cols: e.dma_start(out=BYs.rearrange("(c p) t -> p c t", p=128)[:, :, cols], in_=byst[:]),
                          reads=[bystr], writes=[R["BYs"]], dma=True)
                    if ti + 1 < NT:
                        rmsnorm_apply(order[ti + 1], gcol, P, *rsl[(ti + 1) % 2])
                    else:
                        pre_norm(nxt, P)
                    def stage1(b, g, par):
                        p0, p1 = par * 64, (par + 1) * 64
                        pis = []
                        for kb in range(3):
                            sbk, sbkr = nbank()
                            S.add("pe", lambda e, sbk=sbk, kb=kb: e.matmul(
                                sbk[:, 0:256], lhsT=KW[p0:p1, g, (b + kb) * 128:(b + kb + 1) * 128],
                                rhs=Qsw[p0:p1, 2 * g:2 * g + 2, b * 128:(b + 1) * 128], start=True, stop=True),
                                reads=[KWr, Qswr], writes=[sbkr])
                            pi = pcnt[0] % NPB
                            pcnt[0] += 1
                            pis.append(pi)
                            S.add("act", lambda e, sbk=sbk, pi=pi: e.activation(out=Pb[pi][:], in_=sbk[:, 0:256], func=AF.Exp, scale=0.125),
                                  reads=[sbkr], writes=[Pbr[pi]])
                            if kb != 1:
                                mi = 0 if kb == 0 else 1
                                S.add("dve", lambda e, pi=pi, mi=mi: e.tensor_tensor(
                                    out=Pb[pi][:].rearrange("p (j q) -> p j q", j=2), in0=Pb[pi][:].rearrange("p (j q) -> p j q", j=2),
                                    in1=masks[:, mi, :].unsqueeze(1).broadcast_to([128, 2, 128]), op=ALU.mult),
                                    reads=[Pbr[pi], Cr], writes=[Pbr[pi]])
                        return pis

                    def stage2(b, g, par, pis):
                        p0, p1 = par * 64, (par + 1) * 64
                        ob, obr = nbank()
                        for kb in range(3):
                            S.add("pe", lambda e, kb=kb, pi=pis[kb]: e.matmul(
                                ob[:, 0:256], lhsT=VW[:, b + kb, g * 128:(g + 1) * 128], rhs=Pb[pi][:], start=(kb == 0), stop=False),
                                reads=[VWr, Pbr[pis[kb]]], writes=[obr])
                        S.add("pe", lambda e: e.matmul(ob[:, 0:256], lhsT=Esk[:], rhs=rsel[:, g * 2 + par, :], start=False, stop=True),
                              reads=[Eskr, Cr], writes=[obr])
                        ri = rcnt[0] % 4
                        rcnt[0] += 1
                        S.add("act", lambda e: e.activation(out=rsb[ri][64:128, :], in_=ob[64:128, 0:256], func=AF.Ln), reads=[obr], writes=[rsbr[ri]])
                        S.add("act", lambda e: e.activation(out=rsb[ri][64:128, :], in_=rsb[ri][64:128, :], func=AF.Exp, scale=-1.0), reads=[rsbr[ri]], writes=[rsbr[ri]])
                        S.add("dve", lambda e: e.tensor_tensor(
                            out=Osw[p0:p1, 2 * g:2 * g + 2, b * 128:(b + 1) * 128], in0=ob[0:64, 0:256].rearrange("p (j q) -> p j q", j=2),
                            in1=rsb[ri][64:128, :].rearrange("p (j q) -> p j q", j=2), op=ALU.mult),
                            reads=[obr, rsbr[ri]], writes=[Oswr])

                    prev = None
                    for b in range(4):
                        for g in range(2):
                            for par in range(2):
                                pis = stage1(b, g, par)
                                if prev is not None:
                                    stage2(*prev)
                                prev = (b, g, par, pis)
                    stage2(*prev)
                    S.add(STQ, lambda e, cols=cols: e.dma_start(out=OSWs.rearrange("(c p) t -> p c t", p=128)[:, :, cols], in_=Osw[:]),
                          reads=[Oswr], writes=[R["OSWs"]], dma=True)
                S.barrier()

        def pass2b(l, nxt=None):
            with contextlib.ExitStack() as ps:
                P = {}
                norm_bufs(ps, P)
                gcol = G_MIX + l * 8
                gs = slab_stream(ps, "wg", "inG", l, 8, [128, 8, 384], 2, 1)
                ys = slab_stream(ps, "wy", "wy", l, 8, [128, 12, 128], 2, 1)
                os_ = slab_stream(ps, "wo", "wo", l, 8, [128, 8, 128], 3, 2)
                yin = sb("yin", [128, 12, TT], BF16, ps); yinr = Res("yin")
                sg = [sb("sg%d" % i, [128, 3, TT], F32, ps) for i in range(2)]
                sgr = [Res("sg%d" % i) for i in range(2)]
                mt = [sb("mt%d" % i, [128, TT], F32, ps) for i in range(3)]
                mtr = [Res("mt%d" % i) for i in range(3)]
                merged = sb("merged", [128, 8, TT], BF16, ps)
                mergedr = [Res("merged%d" % i) for i in range(8)]
                for t in range(NT):
                    cols = slice(t * TT, (t + 1) * TT)
                    gs.get(t * 8)
                    ys.get(t * 8)
                    for i, (src, rn) in enumerate(((Os, "Os"), (BYs, "BYs"), (OSWs, "OSWs"))):
                        S.add("sp", lambda e, i=i, src=src, cols=cols: e.dma_start(out=yin[:, 4 * i:4 * i + 4, :], in_=src.rearrange("(c p) t -> p c t", p=128)[:, :, cols]),
                              reads=[R[rn]], writes=[yinr], dma=True)
                    if t == 0:
                        first_norm(0, gcol, P)
                    for m in range(8):
                        gb, gbr = gs.get(t * 8 + m)
                        yb_, ybr_ = ys.get(t * 8 + m)
                        if m == 6:
                            os_.get(t * 8)
                        si = m % 2
                        gbk = []
                        for q in range(3):
                            bk, bkr = nbank()
                            for k in range(8):
                                S.add("pe", lambda e, k=k, q=q, bk=bk, gb=gb: e.matmul(bk, lhsT=gb[:, k, q * 128:(q + 1) * 128], rhs=P["xn"][:, k, :], start=(k == 0), stop=(k == 7)),
                                      reads=[gbr, P["xnr"][k]], writes=[bkr])
                            S.add("act", lambda e, q=q, bk=bk, si=si: e.activation(out=sg[si][:, q, :], in_=bk, func=AF.Sigmoid), reads=[bkr], writes=[sgr[si]])
                        for q in range(3):
                            bk, bkr = nbank()
                            for k in range(4):
                                S.add("pe", lambda e, k=k, q=q, bk=bk, yb_=yb_: e.matmul(bk, lhsT=yb_[:, 4 * q + k, :], rhs=yin[:, 4 * q + k, :], start=(k == 0), stop=(k == 3)),
                                      reads=[ybr_, yinr], writes=[bkr])
                            S.add("dve", lambda e, q=q, bk=bk, si=si: e.tensor_tensor(out=mt[q][:], in0=bk, in1=sg[si][:, q, :], op=ALU.mult),
                                  reads=[bkr, sgr[si]], writes=[mtr[q]])
                        S.add("dve", lambda e: e.tensor_tensor(out=mt[0][:], in0=mt[0][:], in1=mt[1][:], op=ALU.add), reads=[mtr[0], mtr[1]], writes=[mtr[0]])
                        S.add("dve", lambda e, m=m: e.tensor_tensor(out=merged[:, m, :], in0=mt[0][:], in1=mt[2][:], op=ALU.add), reads=[mtr[0], mtr[2]], writes=[mergedr[m]])
                    if t + 1 < NT:
                        rmsnorm_tile(t + 1, gcol, P)
                    else:
                        pre_norm(nxt, P)
                    for m in range(8):
                        ob_, obr_ = os_.get(t * 8 + m)
                        bk, bkr = nbank()
                        for k in range(8):
                            S.add("pe", lambda e, k=k, bk=bk, ob_=ob_: e.matmul(bk, lhsT=ob_[:, k, :], rhs=merged[:, k, :], start=(k == 0), stop=(k == 7)),
                                  reads=[obr_, mergedr[k]], writes=[bkr])
                        S.add("dve", lambda e, m=m, bk=bk, cols=cols: e.tensor_tensor(out=X[:, m, cols], in0=bk, in1=X[:, m, cols], op=ALU.add),
                              reads=[bkr, Xr[m][t]], writes=[Xr[m][t]])
                S.barrier()

        def final_phase():
            gcol = G_FIN
            with contextlib.ExitStack() as ps:
                P = {}
                P["sq"] = [sb("fsq%d" % i, [128, TT], BF16, ps) for i in range(2)]
                P["sqr"] = [Res("fsq%d" % i) for i in range(2)]
                rsb = [sb("frstd%d" % i, [128, TT], F32, ps) for i in range(2)]
                rsbr = [Res("frstd%d" % i) for i in range(2)]
                yb = [sb("yb%d" % i, [128, NC8, TT], F32, ps) for i in range(2)]
                ybr = [Res("yb%d" % i) for i in range(2)]
                for t in range(NT):
                    cols = slice(t * TT, (t + 1) * TT)
                    rs, rsr = rsb[t % 2], rsbr[t % 2]
                    norm_stat([X[:, c, cols] for c in range(NC8)], [Xr[c][t] for c in range(NC8)], D, rs, rsr, P)
                    y, yr = yb[t % 2], ybr[t % 2]
                    for c in range(NC8):
                        S.add("dve", lambda e, c=c, y=y, rs=rs, cols=cols: e.scalar_tensor_tensor(
                            out=y[:, c, :], in0=X[:, c, cols], scalar=G[:, gcol + c:gcol + c + 1], in1=rs[:],
                            op0=ALU.mult, op1=ALU.mult),
                            reads=[Xr[c][t], Cr, rsr], writes=[yr])
                    S.add("sp", lambda e, y=y, cols=cols: e.dma_start(
                        out=yT.rearrange("(c p) t -> p c t", p=128)[:, :, cols], in_=y[:]),
                        reads=[yr], dma=True)
                S.barrier()

        order = ["gu1", "dn1", "inA", "inB", "inG", "wy", "wo", "gu2", "dn2"]
        need = set()
        if cfg.ffn1:
            need |= {"gu1", "dn1"}
        if cfg.ffn2:
            need |= {"gu2", "dn2"}
        if cfg.mixer:
            need |= {"inA", "inB", "inG", "wy", "wo"}
        def cast_layer(l):
            for nm in order:
                if nm in need:
                    cast_w(nm, l)
            ukvres[l] = Res("wukv_%d" % l)
            S.add("pool", lambda e: e.dma_start(out=wukv_b[l], in_=wukv_d[l]), writes=[ukvres[l]], dma=True)

        cast_layer(0)
        for l in range(cfg.layers):
            full = cfg.ffn1 and cfg.mixer and cfg.p2a and cfg.mla and cfg.p2b and cfg.ffn2
            gm = G_MIX + l * 8
            if cfg.ffn1:
                ffn_phase(l, 1, (0, gm) if full else None)
            if cfg.mixer:
                pass1(l, (1, gm) if full else None)
                exchange_start(l)
                if cfg.p2a:
                    pass2a(l, (0, gm) if full else None)
                if l + 1 < cfg.layers:
                    cast_layer(l + 1)
                if cfg.mla:
                    mla_phase(l)
                if cfg.p2b:
                    pass2b(l, (0, G_F2 + l * 8) if full else None)
            if cfg.ffn2:
                ffn_phase(l, 2, (0, G_F1 + (l + 1) * 8) if (full and l + 1 < cfg.layers) else None)
        if cfg.final:
            final_phase()
        else:
            for t in range(NT):
                S.add("sp", lambda e, t=t: e.dma_start(
                    out=yT.rearrange("(c p) t -> p c t", p=128)[:, :, t * TT:(t + 1) * TT],
                    in_=X[:, :, t * TT:(t + 1) * TT]),
                    reads=[Xr[c][t] for c in range(NC8)], dma=True)
        S.emit(nc, es)
    return nc


_NC_CACHE = {}


def kernel(**inputs):
    cfg = Cfg()
    if "nc" not in _NC_CACHE:
        _NC_CACHE["nc"] = build(cfg)
    nc = _NC_CACHE["nc"]
    in_maps = prep_inputs(inputs)
    res = run_bass_kernel_spmd(nc, in_maps, core_ids=list(range(8)))
    outs = [np.ascontiguousarray(np.asarray(r["yT"]).T) for r in res.results]
    y_prompt = np.stack(outs[:4], axis=0)
    y_sample = np.stack([np.concatenate([outs[4], outs[5]], axis=0),
                         np.concatenate([outs[6], outs[7]], axis=0)], axis=0)
    return (y_prompt.astype(np.float32), y_sample.astype(np.float32))
```
